# Optimizing a Trainium2 kernel written in Bass

```python
import jax, jax.numpy as jnp
from jax import lax
import numpy as np

D_MODEL = 1024
BATCH = 8
SEQ = 4096
DEPTH = 1

N_MEM = 256
HG_HEADS = 4
HG_DK = 128
HG_DV = 128
ML_HEADS = 4
ML_DK = 128
ML_DV = 128
D_HG = HG_HEADS * HG_DV
D_ML = ML_HEADS * ML_DV
D_MIX = D_HG + D_ML
CHUNK = 64
ML_CONV = 4
FFN_CONV = 3
D_FF = 2816
CA_HEADS = 4
CA_DH = D_MODEL // CA_HEADS
ALPHA = (2.0 * DEPTH) ** 0.25
BETA = (8.0 * DEPTH) ** -0.25
LN_EPS = 1e-5
NEG_BIG = -1e30

IN_SIZES = (HG_HEADS * HG_DK, HG_HEADS * HG_DK, D_HG, D_HG,
            ML_HEADS * ML_DK, ML_HEADS * ML_DK, D_ML, D_ML, ML_HEADS, ML_HEADS)
IN_SPLITS = tuple(int(c) for c in np.cumsum(IN_SIZES)[:-1])
D_IN = int(sum(IN_SIZES))
FG_START = int(sum(IN_SIZES[:-1]))

kernel_name = 'hybrid_hgrn2_mlstm_deepnorm'


def layer_norm(x, g, b):
    xf = x.astype(jnp.float32)
    mu = jnp.mean(xf, axis=-1, keepdims=True)
    var = jnp.mean(jnp.square(xf - mu), axis=-1, keepdims=True)
    return ((xf - mu) * lax.rsqrt(var + LN_EPS) * g + b).astype(x.dtype)


def head_rms_norm(h, w):
    y = h * lax.rsqrt(jnp.mean(h * h, axis=-1, keepdims=True) + LN_EPS)
    return y.reshape(*h.shape[:-2], -1) * w


def head_layer_norm(h, w):
    mu = jnp.mean(h, axis=-1, keepdims=True)
    var = jnp.mean(jnp.square(h - mu), axis=-1, keepdims=True)
    y = (h - mu) * lax.rsqrt(var + LN_EPS)
    return y.reshape(*h.shape[:-2], -1) * w


def causal_dwconv(x, w, b):
    k_w = w.shape[0]
    s = x.shape[1]
    xp = jnp.pad(x, ((0, 0), (k_w - 1, 0), (0, 0)))
    y = b
    for j in range(k_w):
        y = y + xp[:, j:j + s] * w[j]
    return y


def to_chunks(t):
    bsz, s, h = t.shape[:3]
    t = t.reshape(bsz, s // CHUNK, CHUNK, h, *t.shape[3:])
    return jnp.moveaxis(jnp.moveaxis(t, 1, 0), 3, 2)


def from_chunks(t):
    t = jnp.moveaxis(jnp.moveaxis(t, 2, 3), 0, 1)
    return t.reshape(t.shape[0], t.shape[1] * t.shape[2], *t.shape[3:])


def hgrn2_chunkwise(q, k, v, log_f):
    bsz, _, h, dk = q.shape
    dv = v.shape[-1]
    mask = jnp.tril(jnp.ones((CHUNK, CHUNK), dtype=bool))

    def step(state, inp):
        q_, k_, v_, lf = inp
        b = jnp.cumsum(lf, axis=2)
        b_ref = b[:, :, CHUNK // 2 - 1:CHUNK // 2]
        attn = jnp.einsum('bhtd,bhsd->bhts', q_ * jnp.exp(b - b_ref), k_ * jnp.exp(b_ref - b))
        attn = jnp.where(mask, attn, 0.0)
        o = (jnp.einsum('bhts,bhsv->bhtv', attn, v_)
             + jnp.einsum('bhtd,bhdv->bhtv', q_ * jnp.exp(b), state))
        b_last = b[:, :, -1:]
        state = (jnp.exp(b_last)[:, :, 0, :, None] * state
                 + jnp.einsum('bhsd,bhsv->bhdv', k_ * jnp.exp(b_last - b), v_))
        return state, o

    s0 = jnp.zeros((bsz, h, dk, dv), jnp.float32)
    _, o = lax.scan(step, s0, (to_chunks(q), to_chunks(k), to_chunks(v), to_chunks(log_f)))
    return from_chunks(o)


def mlstm_chunkwise(q, k, v, i_log, f_log):
    bsz, _, h, dk = q.shape
    dv = v.shape[-1]
    mask = jnp.tril(jnp.ones((CHUNK, CHUNK), dtype=bool))

    def step(carry, inp):
        c_st, n_st, m_st = carry
        q_, k_, v_, ig, lf = inp
        b = jnp.cumsum(lf, axis=-1)
        g = b[..., -1]
        d = jnp.where(mask, b[..., :, None] - b[..., None, :] + ig[..., None, :], -jnp.inf)
        inter = b + m_st[..., None]
        m_t = jnp.maximum(inter, jnp.max(d, axis=-1))
        w = jnp.exp(d - m_t[..., None])
        s = jnp.einsum('bhtd,bhsd->bhts', q_, k_) * w
        w_inter = jnp.exp(inter - m_t)
        num = (jnp.einsum('bhts,bhsv->bhtv', s, v_)
               + w_inter[..., None] * jnp.einsum('bhtd,bhdv->bhtv', q_, c_st))
        den = jnp.sum(s, axis=-1) + w_inter * jnp.einsum('bhtd,bhd->bht', q_, n_st)
        h_out = num / jnp.maximum(jnp.abs(den), jnp.exp(-m_t))[..., None]
        a = g[..., None] - b + ig
        m_new = jnp.maximum(g + m_st, jnp.max(a, axis=-1))
        decay = jnp.exp(g + m_st - m_new)
        wk = k_ * jnp.exp(a - m_new[..., None])[..., None]
        c_st = decay[..., None, None] * c_st + jnp.einsum('bhsd,bhsv->bhdv', wk, v_)
        n_st = decay[..., None] * n_st + jnp.sum(wk, axis=2)
        return (c_st, n_st, m_new), h_out

    init = (jnp.zeros((bsz, h, dk, dv), jnp.float32),
            jnp.zeros((bsz, h, dk), jnp.float32),
            jnp.full((bsz, h), NEG_BIG, jnp.float32))
    _, o = lax.scan(step, init, (to_chunks(q), to_chunks(k), to_chunks(v),
                                 to_chunks(i_log), to_chunks(f_log)))
    return from_chunks(o)


def hybrid_mixer(x, w_in, b_in, lb, hg_norm_w, ml_conv_w, ml_conv_b, ml_norm_w, w_out):
    bsz, s, _ = x.shape
    proj = x @ w_in + b_in
    hq, hf, hi, hg, mq, mk, mv, mo, mi, mf = jnp.split(proj, IN_SPLITS, axis=-1)
    f32 = lambda t: t.astype(jnp.float32)
    heads = lambda t, nh: t.reshape(bsz, s, nh, -1)
    sig = jax.nn.sigmoid(f32(hf))
    log_f = jnp.log(lb + (1.0 - lb) * sig)
    k_in = (1.0 - lb) * jax.nn.sigmoid(-f32(hf))
    o_hg = hgrn2_chunkwise(heads(jax.nn.silu(f32(hq)), HG_HEADS), heads(k_in, HG_HEADS),
                           heads(f32(hi), HG_HEADS), heads(log_f, HG_HEADS))
    o_hg = head_rms_norm(o_hg, hg_norm_w) * jax.nn.silu(f32(hg))
    qk = jax.nn.silu(f32(causal_dwconv(jnp.concatenate([mq, mk], axis=-1), ml_conv_w, ml_conv_b)))
    q_ml, k_ml = jnp.split(qk, 2, axis=-1)
    h_ml = mlstm_chunkwise(heads(q_ml, ML_HEADS) * (ML_DK ** -0.5), heads(k_ml, ML_HEADS),
                           heads(f32(mv), ML_HEADS), f32(mi), jax.nn.log_sigmoid(f32(mf)))
    o_ml = jax.nn.sigmoid(f32(mo)) * head_layer_norm(h_ml, ml_norm_w)
    y = jnp.concatenate([o_hg, o_ml], axis=-1).astype(x.dtype)
    return y @ w_out


def memory_cross_attention(x, mem, wq, wkv, wo):
    bsz, s, d = x.shape
    q = (x @ wq).reshape(bsz, s, CA_HEADS, CA_DH)
    k, v = jnp.split(mem @ wkv, 2, axis=-1)
    k = k.reshape(bsz, -1, CA_HEADS, CA_DH)
    v = v.reshape(bsz, -1, CA_HEADS, CA_DH)
    sc = jnp.einsum('bshd,bmhd->bhsm', q, k).astype(jnp.float32) * (CA_DH ** -0.5)
    p = jax.nn.softmax(sc, axis=-1).astype(v.dtype)
    o = jnp.einsum('bhsm,bmhd->bshd', p, v).reshape(bsz, s, d)
    return o @ wo


def conv_ffn(x, w_up, conv_w, conv_b, w_down):
    u = causal_dwconv(x @ w_up, conv_w, conv_b)
    gate, val = jnp.split(u, 2, axis=-1)
    return (jax.nn.gelu(gate) * val) @ w_down


def setup_inputs(seed: int = 0) -> dict:
    key = jax.random.key(seed)
    ks = jax.random.split(key, 24)
    nrm = lambda k, shape, scale: jax.random.normal(k, shape, jnp.float32) * scale
    b_in = nrm(ks[3], (DEPTH, D_IN), 0.02)
    b_in = b_in.at[:, FG_START:].add(jnp.linspace(3.0, 6.0, ML_HEADS, dtype=jnp.float32))
    return {
        'x': nrm(ks[0], (BATCH, SEQ, D_MODEL), 1.0),
        'mem': nrm(ks[1], (BATCH, N_MEM, D_MODEL), 1.0),
        'w_in': nrm(ks[2], (DEPTH, D_MODEL, D_IN), D_MODEL ** -0.5),
        'b_in': b_in,
        'hg_lb_logits': 1.0 + nrm(ks[4], (DEPTH + 1, D_HG), 0.3),
        'hg_norm_w': 1.0 + nrm(ks[5], (DEPTH, D_HG), 0.02),
        'ml_conv_w': nrm(ks[6], (DEPTH, ML_CONV, 2 * D_ML), ML_CONV ** -0.5),
        'ml_conv_b': nrm(ks[7], (DEPTH, 2 * D_ML), 0.02),
        'ml_norm_w': 1.0 + nrm(ks[8], (DEPTH, D_ML), 0.02),
        'w_out': nrm(ks[9], (DEPTH, D_MIX, D_MODEL), BETA * D_MIX ** -0.5),
        'ln1_g': 1.0 + nrm(ks[10], (DEPTH, D_MODEL), 0.02),
        'ln1_b': nrm(ks[11], (DEPTH, D_MODEL), 0.02),
        'ca_wq': nrm(ks[12], (DEPTH, D_MODEL, D_MODEL), D_MODEL ** -0.5),
        'ca_wkv': nrm(ks[13], (DEPTH, D_MODEL, 2 * D_MODEL), D_MODEL ** -0.5),
        'ca_wo': nrm(ks[14], (DEPTH, D_MODEL, D_MODEL), BETA * D_MODEL ** -0.5),
        'ln2_g': 1.0 + nrm(ks[15], (DEPTH, D_MODEL), 0.02),
        'ln2_b': nrm(ks[16], (DEPTH, D_MODEL), 0.02),
        'ffn_w_up': nrm(ks[17], (DEPTH, D_MODEL, 2 * D_FF), D_MODEL ** -0.5),
        'ffn_conv_w': nrm(ks[18], (DEPTH, FFN_CONV, 2 * D_FF), FFN_CONV ** -0.5),
        'ffn_conv_b': nrm(ks[19], (DEPTH, 2 * D_FF), 0.02),
        'ffn_w_down': nrm(ks[20], (DEPTH, D_FF, D_MODEL), BETA * D_FF ** -0.5),
        'ln3_g': 1.0 + nrm(ks[21], (DEPTH, D_MODEL), 0.02),
        'ln3_b': nrm(ks[22], (DEPTH, D_MODEL), 0.02),
    }


def reference(x, mem, w_in, b_in, hg_lb_logits, hg_norm_w, ml_conv_w, ml_conv_b, ml_norm_w,
              w_out, ln1_g, ln1_b, ca_wq, ca_wkv, ca_wo, ln2_g, ln2_b,
              ffn_w_up, ffn_conv_w, ffn_conv_b, ffn_w_down, ln3_g, ln3_b):
    lower_bounds = jnp.cumsum(jax.nn.softmax(hg_lb_logits.astype(jnp.float32), axis=0), axis=0)
    for l in range(DEPTH):
        mix = hybrid_mixer(x, w_in[l], b_in[l], lower_bounds[l], hg_norm_w[l],
                           ml_conv_w[l], ml_conv_b[l], ml_norm_w[l], w_out[l])
        x = layer_norm(ALPHA * x + mix, ln1_g[l], ln1_b[l])
        ca = memory_cross_attention(x, mem, ca_wq[l], ca_wkv[l], ca_wo[l])
        x = layer_norm(ALPHA * x + ca, ln2_g[l], ln2_b[l])
        ff = conv_ffn(x, ffn_w_up[l], ffn_conv_w[l], ffn_conv_b[l], ffn_w_down[l])
        x = layer_norm(ALPHA * x + ff, ln3_g[l], ln3_b[l])
    return x
```

```python
import numpy as np
import concourse.bass as bass
import concourse.mybir as mybir
from concourse.bass_utils import run_bass_kernel_spmd

F32 = mybir.dt.float32
BF16 = mybir.dt.bfloat16
AF = mybir.ActivationFunctionType
ALU = mybir.AluOpType
AX = mybir.AxisListType

S = 4096
D = 1024
TM = 512
NTILES = S // TM
ALPHA = 2.0 ** 0.25
EPS = 1e-5
NPP = 256
R_HGW, R_MLW, R_LN, R_B = 0, 512, 1024, 1024 + 6 * 1024
NR = R_B + 2048


class Eng:
    def __init__(self, name, eng, sem, inc=1):
        self.name, self.eng, self.sem, self.inc = name, eng, sem, inc
        self.count = 0
        self.waited = {}


class T:
    def __init__(self, h, name=None):
        self.h = h
        self.name = name
        self.w = None
        self.r = {}
        self.dma = None
        self.psum = False
        self.group = [self]

    def __getitem__(self, k):
        return self.h[k]


class TV:
    def __init__(self, parent, ap):
        self.__dict__["p"] = parent
        self.__dict__["h"] = ap

    def __getitem__(self, k):
        return self.h[k]

    def __getattr__(self, k):
        return getattr(self.__dict__["p"], k)

    def __setattr__(self, k, v):
        setattr(self.__dict__["p"], k, v)


def alias(*ts):
    g = []
    for t in ts:
        for u in t.group:
            if u not in g:
                g.append(u)
    for t in g:
        t.group = g


class FW:
    def __init__(self, nc):
        self.nc = nc
        self.pe = self._mk("pe", nc.tensor)
        self.act = self._mk("act", nc.scalar)
        self.dve = self._mk("dve", nc.vector)
        self.pool = self._mk("pool", nc.gpsimd)
        self.sp = self._mk("sp", nc.sync)
        self.ndma = 0

    def _mk(self, name, eng, inc=1):
        sem = self.nc.semaphore(name).__enter__()
        return Eng(name, eng, sem, inc)

    def sb(self, name, shape, dt):
        return T(self.nc.alloc_sbuf_tensor("sb_" + name, list(shape), dt), name)

    def ps(self, name, shape, dt=F32):
        t = T(self.nc.alloc_psum_tensor("ps_" + name, list(shape), dt), name)
        t.psum = True
        return t

    def _deps(self, reads, writes, E=None):
        deps = {}

        def add(e, c):
            if deps.get(e, 0) < c:
                deps[e] = c
        for t0 in reads:
            for t in t0.group:
                if t.w:
                    add(*t.w)
                if t.psum:
                    for e, c in t.r.items():
                        if e is not E:
                            add(e, c)
        for t0 in writes:
            for t in t0.group:
                if t.w:
                    add(*t.w)
                for e, c in t.r.items():
                    add(e, c)
        return deps

    def _wait(self, E, deps, skip_self=False):
        for e, c in deps.items():
            if e is E and skip_self:
                continue
            if E.waited.get(e, 0) < c:
                E.eng.wait_ge(e.sem, c * e.inc)
                E.waited[e] = c

    def op(self, E, fn, reads=(), writes=()):
        deps = self._deps(reads, writes, E)
        self._wait(E, deps, skip_self=(E is self.pe))
        inst = fn()
        E.count += 1
        inst.then_inc(E.sem, 1)
        for t in reads:
            if t.r.get(E, 0) < E.count:
                t.r[E] = E.count
        for t in writes:
            t.w = (E, E.count)
            t.r = {}
        return inst

    def dma(self, E, out_t, out_ap, in_t, in_ap, **kw):
        tgt = out_t if out_t is not None else in_t
        if tgt.dma is None:
            tgt.dma = self._mk("dma%d" % self.ndma, None, inc=16)
            self.ndma += 1
        Dq = tgt.dma
        reads = [in_t] if in_t is not None else []
        writes = [out_t] if out_t is not None else []
        deps = self._deps(reads, writes)
        self._wait(E, deps)
        inst = E.eng.dma_start(out=out_ap, in_=in_ap, **kw)
        Dq.count += 1
        inst.then_inc(Dq.sem, 16)
        for t in reads:
            if t.r.get(Dq, 0) < Dq.count:
                t.r[Dq] = Dq.count
        for t in writes:
            t.w = (Dq, Dq.count)
            t.r = {}
        return inst

    def finish(self, E, ts):
        deps = {}
        for t in ts:
            if t.w and deps.get(t.w[0], 0) < t.w[1]:
                deps[t.w[0]] = t.w[1]
        for e, c in deps.items():
            E.eng.wait_ge(e.sem, c * e.inc)


class _Stop(Exception):
    pass


def build(ntiles=NTILES, dumps=(), stop=None):
    nc = bass.Bass("TRN2", target_bir_lowering=False)
    f = FW(nc)
    try:
        _build_body(nc, f, ntiles, dumps, stop)
    except _Stop:
        pass
    f.finish(f.sp, f.final_ts)
    return nc


def _build_body(nc, f, ntiles, dumps, stop):
    def chk(tag):
        if stop == tag:
            raise _Stop()
    pe, act, dve, pool, sp = f.pe, f.act, f.dve, f.pool, f.sp
    V, Sx, Tn, G = nc.vector, nc.scalar, nc.tensor, nc.gpsimd

    def din(name, shape, dt=F32):
        return nc.dram_tensor(name, list(shape), dt, kind="ExternalInput").ap()

    x_d = din("x", [S, D])
    mem_d = din("mem", [256, D])
    whd_d = din("whd", [8, 128, 4096])
    wg_d = din("wg", [128, 64])
    wsq_d = din("wsq", [3, 128, 8192])
    wkv_d = din("wkv", [2, 128, 8192])
    wup_d = din("wup", [11, 128, 4096])
    wdn_d = din("wdn", [2, 11, 128, 1024])
    pp_d = din("pp", [128, NPP])
    rows_d = din("rows", [1, NR])
    cst_d = din("cst", [128, 512])
    out_d = nc.dram_tensor("out", [S, D], F32, kind="ExternalOutput").ap()
    out_t = T(None, "out")
    dump_ts = [out_t]
    f.final_ts = dump_ts

    def scratch(name, shape):
        return nc.dram_tensor(name, list(shape), BF16, kind="Internal").ap()
    whd_s = scratch("whd_s", [8, 128, 4096]); whd_st = T(None, "whd_s")
    wsq_s = scratch("wsq_s", [3, 128, 8192]); wsq_st = T(None, "wsq_s")
    wkv_s = scratch("wkv_s", [2, 128, 8192]); wkv_st = T(None, "wkv_s")
    wup_s = scratch("wup_s", [11, 128, 4096]); wup_st = T(None, "wup_s")
    wdn_s = scratch("wdn_s", [2, 11, 128, 1024]); wdn_st = T(None, "wdn_s")

    whd_sts = [T(None, "whd_s%d" % h) for h in range(8)]
    for i in range(2):
        f.dma(pool, wkv_st, wkv_s[i], None, wkv_d[i])
    for h in range(8):
        f.dma(pool, whd_sts[h], whd_s[h], None, whd_d[h])
    for i in range(3):
        f.dma(pool, wsq_st, wsq_s[i], None, wsq_d[i])
    for g in range(11):
        f.dma(pool, wup_st, wup_s[g], None, wup_d[g])
        for hf in range(2):
            f.dma(pool, wdn_st, wdn_s[hf, g], None, wdn_d[hf, g])

    chk("prologue")
    cst = f.sb("cst", [128, 512], F32)
    f.dma(sp, cst, cst[:, :], None, cst_d)
    ident = cst[:, 0:128]
    maskT = cst[:, 128:256]
    selm = cst[0:4, 256:288]
    identb = f.sb("identb", [128, 128], BF16)
    f.dma(pool, identb, identb[:, :], None, cst_d[:, 0:128])
    pp = f.sb("pp", [128, NPP], F32)
    f.dma(sp, pp, pp[:, :], None, pp_d)
    wg = f.sb("wg", [128, 8, 8], BF16)
    f.dma(pool, wg, wg[:, :, :], None, wg_d.rearrange("p (k c) -> p k c", c=8))
    ones = f.sb("ones", [128, 512], F32)
    f.op(pool, lambda: G.memset(ones[:, :], 1.0), writes=[ones])
    onesb = f.sb("onesb", [128, 128], BF16)
    f.op(pool, lambda: G.memset(onesb[:, :], 1.0), writes=[onesb])
    hgw = f.sb("hgw", [128, 512], F32)
    f.dma(sp, hgw, hgw[:, :], None, rows_d[0:1, R_HGW:R_HGW + 512].partition_broadcast(128))
    mlw = f.sb("mlw", [128, 512], F32)
    f.dma(sp, mlw, mlw[:, :], None, rows_d[0:1, R_MLW:R_MLW + 512].partition_broadcast(128))
    browb = [f.sb("brow%d" % i, [1, 256], F32) for i in range(2)]
    btmp = f.sb("btmp", [1, 256], F32)
    bhl_all = f.sb("bhl_all", [1, 8, 2, 256], BF16)
    for hd_ in range(8):
        brow_ = browb[hd_ % 2]
        f.dma(sp, brow_, brow_[:, :], None, rows_d[0:1, R_B + hd_ * 256:R_B + (hd_ + 1) * 256])
        f.op(dve, lambda: nc.vector.tensor_copy(bhl_all[:, hd_, 0, :], brow_[:, :]), reads=[brow_], writes=[bhl_all])
        f.op(dve, lambda: nc.vector.tensor_copy(btmp[:, :], bhl_all[:, hd_, 0, :]), reads=[bhl_all], writes=[btmp])
        f.op(dve, lambda: nc.vector.tensor_tensor(bhl_all[:, hd_, 1, :], brow_[:, :], btmp[:, :], ALU.subtract), reads=[brow_, btmp], writes=[bhl_all])
    lnp = f.sb("lnp", [128, 2, 1024], F32)

    pd = f.sb("pd", [128, 16], F32)
    for h in range(4):
        c = h * 4
        f.op(dve, lambda: V.tensor_tensor(pd[:, 13:14], pp[:, c + 2:c + 3], pp[:, c + 3:c + 4], ALU.subtract), reads=[pp], writes=[pd])
        f.op(act, lambda: Sx.activation(pd[:, h * 3:h * 3 + 1], pd[:, 13:14], AF.Sigmoid), reads=[pd], writes=[pd])
        f.op(dve, lambda: V.tensor_scalar(pd[:, h * 3 + 1:h * 3 + 2], pd[:, h * 3:h * 3 + 1], -1.0, 1.0, ALU.mult, ALU.add), reads=[pd], writes=[pd])
        f.op(dve, lambda: V.tensor_scalar(pd[:, h * 3 + 2:h * 3 + 3], pd[:, h * 3:h * 3 + 1], 1.0, -1.0, ALU.mult, ALU.add), reads=[pd], writes=[pd])
    f.op(dve, lambda: V.tensor_scalar(pd[:, 12:13], pp[:, 241:242], -1.0, None, ALU.mult), reads=[pp], writes=[pd])

    chk("consts")
    pAB = f.ps("pAB", [128, 1024])
    p2 = f.ps("p2", [128, 512]); p3 = f.ps("p3", [128, 512]); p4 = f.ps("p4", [128, 512])
    p5 = f.ps("p5", [128, 512]); p6 = f.ps("p6", [128, 512])
    p7 = f.ps("p7", [128, 512])
    p7b = TV(p7, p7.h.bitcast(BF16))

    Shg = [f.sb("Shg%d" % h, [128, 128], F32) for h in range(4)]
    Cml = [f.sb("Cml%d" % h, [128, 129], F32) for h in range(4)]
    for h in range(4):
        f.op(pool, lambda: G.memset(Shg[h][:, :], 0.0), writes=[Shg[h]])
        f.op(pool, lambda: G.memset(Cml[h][:, :], 0.0), writes=[Cml[h]])
    mcar = f.sb("mcar", [4, 1], F32)
    f.op(pool, lambda: G.memset(mcar[:, :], -1e30), writes=[mcar])
    halo_ml = f.sb("halo_ml", [128, 8, 3], F32)
    f.op(pool, lambda: G.memset(halo_ml[:, :, :], 0.0), writes=[halo_ml])
    halo_ff = f.sb("halo_ff", [128, 44, 2], F32)
    f.op(pool, lambda: G.memset(halo_ff[:, :, :], 0.0), writes=[halo_ff])

    KT = f.sb("KT", [128, 8, 256], BF16)
    Vm = f.sb("Vm", [128, 2, 1024], BF16)
    xres = f.sb("xres", [128, 4, 1024], F32)
    xr = [T(xres.h[:, s_, :], "xr%d" % s_) for s_ in range(4)]
    smln = [f.sb("smln%d" % s_, [128, 32], F32) for s_ in range(4)]
    xT = f.sb("xT", [128, 8, 512], BF16)
    wsqb = [f.sb("wsq%d" % i, [128, 8, 1024], BF16) for i in range(2)]

    def transposes_to_xT(nsub, src):
        for s in range(nsub):
            if s % 2 == 0:
                for kc in range(8):
                    f.op(pe, lambda: Tn.transpose(pAB[:, kc * 128:(kc + 1) * 128], xr[s][:, kc * 128:(kc + 1) * 128], ident),
                         reads=[xr[s], cst], writes=[pAB])
                f.op(dve, lambda: V.tensor_copy(xT[:, :, s * 128:(s + 1) * 128], pAB[:, :].rearrange("p (k t) -> p k t", t=128)),
                     reads=[pAB], writes=[xT])
            else:
                for kc in range(8):
                    pb = p5 if kc < 4 else p6
                    f.op(pe, lambda: Tn.transpose(pb[:, (kc % 4) * 128:(kc % 4 + 1) * 128], xr[s][:, kc * 128:(kc + 1) * 128], ident),
                         reads=[xr[s], cst], writes=[pb])
                f.op(act, lambda: Sx.copy(xT[:, 0:4, s * 128:(s + 1) * 128], p5[:, :].rearrange("p (k t) -> p k t", t=128)),
                     reads=[p5], writes=[xT])
                f.op(act, lambda: Sx.copy(xT[:, 4:8, s * 128:(s + 1) * 128], p6[:, :].rearrange("p (k t) -> p k t", t=128)),
                     reads=[p6], writes=[xT])

    for s_ in range(2):
        f.dma(sp, xr[s_], xr[s_][:, :], None, mem_d[s_ * 128:(s_ + 1) * 128, :])
    transposes_to_xT(2, None)
    for i in range(2):
        f.dma(sp, wsqb[i], wsqb[i][:, :, :], wkv_st, wkv_s[i].rearrange("p (k c) -> p k c", c=1024))
    for cb in range(8):
        for kc in range(8):
            f.op(pe, lambda: Tn.matmul(p2[:, 0:256], wsqb[0][:, kc, cb * 128:(cb + 1) * 128], xT[:, kc, 0:256], start=(kc == 0), stop=(kc == 7)),
                 reads=[wsqb[0], xT], writes=[p2])
        f.op(dve, lambda: V.tensor_copy(KT[:, cb, :], p2[:, 0:256]), reads=[p2], writes=[KT])
    for mc in range(2):
        for hf in range(2):
            for kc in range(8):
                f.op(pe, lambda: Tn.matmul(p3[:, :], xT[:, kc, mc * 128:(mc + 1) * 128], wsqb[1][:, kc, hf * 512:(hf + 1) * 512], start=(kc == 0), stop=(kc == 7)),
                     reads=[wsqb[1], xT], writes=[p3])
            f.op(act, lambda: Sx.copy(Vm[:, mc, hf * 512:(hf + 1) * 512], p3[:, :]), reads=[p3], writes=[Vm])

    chk("kv")
    Whd = [f.sb("Whd%d" % i, [128, 8, 512], BF16) for i in range(2)]
    yT = f.sb("yT", [128, 8, 512], BF16)
    fa = [f.sb("fa%d" % i, [128, 512], F32) for i in range(6)]
    ubuf = f.sb("ubuf", [128, 520], F32)
    ba = [f.sb("ba%d" % i, [128, 512], BF16) for i in range(5)]
    attb = f.sb("attb", [128, 4, 128], BF16)
    ktok = f.sb("ktok", [128, 4, 128], BF16)
    vtok = f.sb("vtok", [128, 4, 132], BF16)
    gw = f.sb("gw", [128, 4, 128], F32)
    ybf = f.sb("ybf", [128, 4, 128], BF16)
    Sbf = f.sb("Sbf", [128, 8, 132], BF16)
    sm = f.sb("sm", [128, 64], F32)
    sm2 = f.sb("sm2", [128, 64], F32)
    junk = f.sb("junk", [128, 128], F32)
    g_s = f.sb("g_s", [4, 64], F32)
    eftok = f.sb("eftok", [128, 4, 2, 4], F32)
    decbc = f.sb("decbc", [128, 32], F32)
    ddb = f.sb("ddb", [4, 2, 32], BF16)
    hT = f.sb("hT", [128, 22, 512], BF16)
    wupb = [f.sb("wup%d" % i, [128, 2, 8, 256], BF16) for i in range(2)]
    wdnb = [f.sb("wdn%d" % i, [128, 2, 512], BF16) for i in range(3)]
    ub2 = f.sb("ub2", [128, 520], F32)
    hTj = [T(hT.h[:, j, :], "hT%d" % j) for j in range(22)]
    qTc = T(hT.h[:, 0:8, :], "qTc")
    qTc.group = [qTc] + hTj[0:8]
    for j in range(8):
        hTj[j].group = [hTj[j], qTc]
    ubs = [ubuf, ub2, f.sb("ub3", [128, 520], F32), f.sb("ub4", [128, 520], F32)]
    ubs_h = [T(u.h[:, 0:8], "ubh") for u in ubs]
    hff = [T(halo_ff.h[:, b, :], "hff%d" % b) for b in range(44)]
    hml = [T(halo_ml.h[:, b, :], "hml%d" % b) for b in range(8)]
    for t_ in hff:
        t_.w = halo_ff.w
    for t_ in hml:
        t_.w = halo_ml.w
    pup = [p2, p3, p4, p7]
    qpA = f.sb("qpA", [128, 512], BF16)
    qpB = f.sb("qpB", [128, 512], BF16)
    mskA = f.sb("mskA", [128, 512], BF16)
    mskB = f.sb("mskB", [128, 512], BF16)
    f.op(pool, lambda: G.memset(mskA[:, :], 0.0), writes=[mskA])
    f.op(pool, lambda: G.memset(mskB[:, :], 0.0), writes=[mskB])
    for s in range(4):
        f.op(pool, lambda: G.memset(mskA[:, s * 128:s * 128 + 64], 1.0), writes=[mskA])
        f.op(pool, lambda: G.memset(mskB[:, s * 128 + 64:s * 128 + 128], 1.0), writes=[mskB])
    h32 = TV(fa[5], fa[5][:, :].rearrange("p (s d) -> p s d", d=128))
    ETb = [ba[3], ba[4]]

    def dump(name, t, ap, shape):
        d = nc.dram_tensor("dump_" + name, list(shape), ap.dtype, kind="ExternalOutput").ap()
        tt = T(None, "dump_" + name)
        f.dma(sp, tt, d, t, ap)
        dump_ts.append(tt)

    def layernorm_rows(s, ln_idx):
        smx = smln[s]
        X = xr[s]
        st = smx[:, 0:12].rearrange("p (a b) -> p a b", b=6)
        for hf in range(2):
            f.op(dve, lambda: V.bn_stats(st[:, hf, :], X[:, hf * 512:(hf + 1) * 512]), reads=[X], writes=[smx])
        f.op(dve, lambda: V.bn_aggr(smx[:, 12:14], smx[:, 0:12]), reads=[smx], writes=[smx])
        f.op(dve, lambda: V.tensor_scalar(smx[:, 14:15], smx[:, 13:14], EPS, None, ALU.add), reads=[smx], writes=[smx])
        f.op(act, lambda: Sx.activation(smx[:, 15:16], smx[:, 14:15], AF.Ln), reads=[smx], writes=[smx])
        f.op(act, lambda: Sx.activation(smx[:, 16:17], smx[:, 15:16], AF.Exp, scale=-0.5), reads=[smx], writes=[smx])

    def layernorm_apply(s):
        smx = smln[s]
        X = xr[s]
        f.op(dve, lambda: V.tensor_scalar(X[:, :], X[:, :], smx[:, 12:13], smx[:, 16:17], ALU.subtract, ALU.mult), reads=[X, smx], writes=[X])
        f.op(pool, lambda: G.tensor_tensor(X[:, :], X[:, :], lnp[:, 0, :], ALU.mult), reads=[X, lnp], writes=[X])
        f.op(pool, lambda: G.tensor_tensor(X[:, :], X[:, :], lnp[:, 1, :], ALU.add), reads=[X, lnp], writes=[X])

    def proj_res_ln(wbuf, srcT, ln_idx):
        f.dma(sp, lnp, lnp[:, :, :], None,
              rows_d[0:1, R_LN + ln_idx * 2048:R_LN + (ln_idx + 1) * 2048].rearrange("o (a d) -> o a d", a=2).partition_broadcast(128))
        for s in range(4):
            for hf in range(2):
                pb = p2 if hf == 0 else p3
                for kc in range(8):
                    f.op(pe, lambda: Tn.matmul(pb[:, :], srcT[:, kc, s * 128:(s + 1) * 128], wbuf[:, kc, hf * 512:(hf + 1) * 512], start=(kc == 0), stop=(kc == 7)),
                         reads=[srcT, wbuf], writes=[pb])
                f.op(dve, lambda: V.scalar_tensor_tensor(xr[s][:, hf * 512:(hf + 1) * 512], xr[s][:, hf * 512:(hf + 1) * 512], ALPHA, pb[:, :], ALU.mult, ALU.add),
                     reads=[xr[s], pb], writes=[xr[s]])
            layernorm_rows(s, ln_idx)
            if s > 0:
                layernorm_apply(s - 1)
        layernorm_apply(3)

    for it in range(ntiles):
        t0 = it * TM
        for s_ in range(4):
            f.dma(sp, xr[s_], xr[s_][:, :], None, x_d[t0 + s_ * 128:t0 + (s_ + 1) * 128, :])
        transposes_to_xT(4, None)
        chk("xT")
        f.dma(sp, Whd[0], Whd[0][:, :, :], whd_sts[0], whd_s[0].rearrange("p (k c) -> p k c", c=512))

        def partA(hd):
            is_hg = hd < 4
            h = hd % 4
            W = Whd[hd % 2]
            brow = browb[hd % 2]
            pvg = pAB[:, :].rearrange("p (s c) -> p s c", c=256)
            qa, ka, qb, kb = ba[0], ba[1], ba[2], ba[3]
            kT_for_tok = kb if is_hg else ka
            q_inter = qb if is_hg else qa
            NV = 128 if is_hg else 129
            W = Whd[hd % 2]
            if hd + 1 < 8:
                f.dma(sp, Whd[(hd + 1) % 2], Whd[(hd + 1) % 2][:, :, :], whd_sts[hd + 1], whd_s[hd + 1].rearrange("p (k c) -> p k c", c=512))
            else:
                f.dma(sp, wsqb[0], wsqb[0][:, :, :], wsq_st, wsq_s[0].rearrange("p (k c) -> p k c", c=1024))
            is_hg = hd < 4
            h = hd % 4
            brow = browb[hd % 2]
            if hd == 4:
                for gi, pb in ((0, p2), (1, p3)):
                    for kc in range(8):
                        f.op(pe, lambda: Tn.matmul(pb[0:4, :], wg[:, kc, gi * 4:gi * 4 + 4], xT[:, kc, :], start=(kc == 0), stop=(kc == 7)),
                             reads=[wg, xT], writes=[pb])
                t1, nb, u, ee, fl = [TV(fa[i], fa[i][0:4, :]) for i in range(1, 6)]
                f.op(act, lambda: Sx.activation(t1[:, :], p3[0:4, :], AF.Exp, bias=pd[0:4, 12:13], scale=-1.0), reads=[p3, pd], writes=[t1])
                f.op(act, lambda: Sx.activation(t1[:, :], t1[:, :], AF.Ln, bias=1.0, scale=1.0), reads=[t1], writes=[t1])
                f.op(dve, lambda: V.tensor_tensor_scan(nb[:, :], ones[0:4, :], t1[:, :], 0.0, ALU.mult, ALU.add), reads=[ones, t1], writes=[nb])
                nbv = nb[:, :].rearrange("p (c t) -> p c t", t=64)
                f.op(dve, lambda: V.memset(g_s[:, 0:16], 0.0), writes=[g_s])
                f.op(dve, lambda: V.tensor_copy(g_s[:, 1:8], nbv[:, 0:7, 63]), reads=[nb], writes=[g_s])
                f.op(dve, lambda: V.tensor_tensor(nbv, nbv, g_s[:, 0:8].unsqueeze(2).to_broadcast([4, 8, 64]), ALU.subtract), reads=[nb, g_s], writes=[nb])
                f.op(dve, lambda: V.tensor_scalar(g_s[:, 9:16], nbv[:, 0:7, 63], -1.0, None, ALU.mult), reads=[nb], writes=[g_s])
                f.op(dve, lambda: V.scalar_tensor_tensor(u[:, :], p2[0:4, :], pp[0:4, 240:241], nb[:, :], ALU.add, ALU.add), reads=[p2, pp, nb], writes=[u])
                uv = u[:, :].rearrange("p (c t) -> p c t", t=64)
                f.op(dve, lambda: V.tensor_reduce(g_s[:, 16:24], uv, AX.X, ALU.max), reads=[u], writes=[g_s])
                f.op(dve, lambda: V.tensor_tensor_scan(g_s[:, 24:32], g_s[:, 8:16], g_s[:, 16:24], mcar[:, 0:1], ALU.add, ALU.max), reads=[g_s, mcar], writes=[g_s])
                f.op(dve, lambda: V.tensor_copy(g_s[:, 32:33], mcar[:, 0:1]), reads=[mcar], writes=[g_s])
                f.op(dve, lambda: V.tensor_tensor(g_s[:, 33:40], g_s[:, 9:16], g_s[:, 24:31], ALU.add), reads=[g_s], writes=[g_s])
                f.op(dve, lambda: V.tensor_tensor(mcar[:, 0:1], g_s[:, 31:32], nbv[:, 7, 63:64], ALU.subtract), reads=[g_s, nb], writes=[mcar])
                f.op(dve, lambda: V.tensor_tensor(g_s[:, 40:48], g_s[:, 32:40], g_s[:, 24:32], ALU.subtract), reads=[g_s], writes=[g_s])
                f.op(dve, lambda: V.tensor_scalar(g_s[:, 40:48], g_s[:, 40:48], -100.0, None, ALU.max), reads=[g_s], writes=[g_s])
                f.op(act, lambda: Sx.activation(g_s[:, 40:48], g_s[:, 40:48], AF.Exp), reads=[g_s], writes=[g_s])
                Rb = g_s[:, 24:32].unsqueeze(2).to_broadcast([4, 8, 64])
                eev = ee[:, :].rearrange("p (c t) -> p c t", t=64)
                flv = fl[:, :].rearrange("p (c t) -> p c t", t=64)
                f.op(dve, lambda: V.tensor_tensor(eev, uv, Rb, ALU.subtract), reads=[u, g_s], writes=[ee])
                f.op(act, lambda: Sx.activation(ee[:, :], ee[:, :], AF.Exp), reads=[ee], writes=[ee])
                f.op(dve, lambda: V.tensor_tensor(flv, nbv, Rb, ALU.subtract), reads=[nb, g_s], writes=[fl])
                f.op(act, lambda: Sx.activation(fl[:, :], fl[:, :], AF.Exp), reads=[fl], writes=[fl])
                for s in range(4):
                    for q, src in ((0, ee), (1, fl)):
                        f.op(pe, lambda: Tn.transpose(p4[:, (s * 2 + q) * 4:(s * 2 + q) * 4 + 4], src[0:4, s * 128:(s + 1) * 128], cst[0:4, 0:4]),
                             reads=[src, cst], writes=[p4])
                f.op(dve, lambda: V.tensor_copy(eftok[:, :, :, :].rearrange("p s q h -> p (s q h)"), p4[:, 0:32]), reads=[p4], writes=[eftok])
                f.op(dve, lambda: V.tensor_tensor(t1[:, 0:32].rearrange("p (h c) -> p h c", c=8), g_s[:, 40:48].unsqueeze(1).to_broadcast([4, 4, 8]),
                                                  selm.rearrange("p (h c) -> p h c", c=8), ALU.mult), reads=[g_s, cst], writes=[t1])
                f.op(dve, lambda: V.tensor_copy(ddb[:, 0, :], t1[:, 0:32]), reads=[t1], writes=[ddb])
                f.op(dve, lambda: V.tensor_copy(t1[:, 32:64], ddb[:, 0, :]), reads=[ddb], writes=[t1])
                f.op(dve, lambda: V.tensor_tensor(ddb[:, 1, :], t1[:, 0:32], t1[:, 32:64], ALU.subtract), reads=[t1], writes=[ddb])
                f.op(pe, lambda: Tn.matmul(p4[:, 64:96], onesb[0:4, 0:128], ddb[:, 0, :], start=True, stop=False), reads=[onesb, ddb], writes=[p4])
                f.op(pe, lambda: Tn.matmul(p4[:, 64:96], onesb[0:4, 0:128], ddb[:, 1, :], start=False, stop=True), reads=[onesb, ddb], writes=[p4])
                f.op(dve, lambda: V.tensor_copy(decbc[:, :], p4[:, 64:96]), reads=[p4], writes=[decbc])

            for blk, pb in ((0, p2), (1, p3)):
                for kc in range(8):
                    f.op(pe, lambda: Tn.matmul(pb[:, :], W[:, kc, blk * 128:(blk + 1) * 128], xT[:, kc, :], start=(kc == 0), stop=(kc == 7)),
                         reads=[W, xT], writes=[pb])
            chk("h%d_fm" % hd)
            pvg = pAB[:, :].rearrange("p (s c) -> p s c", c=256)
            for s in range(4):
                for kc in range(8):
                    f.op(pe, lambda: Tn.matmul(pvg[:, s, :], xT[:, kc, s * 128:(s + 1) * 128], W[:, kc, 256:512], start=(kc == 0), stop=False),
                         reads=[W, xT], writes=[pAB])
                chk("h%d_tm%d" % (hd, s))
                f.op(pe, lambda: Tn.matmul(pvg[:, s, :], onesb[0:1, 0:128], bhl_all[0:1, hd, 0, :], start=False, stop=False),
                     reads=[onesb, bhl_all], writes=[pAB])
                f.op(pe, lambda: Tn.matmul(pvg[:, s, :], onesb[0:1, 0:128], bhl_all[0:1, hd, 1, :], start=False, stop=True),
                     reads=[onesb, bhl_all], writes=[pAB])


        def partB(hd):
            is_hg = hd < 4
            h = hd % 4
            W = Whd[hd % 2]
            brow = browb[hd % 2]
            pvg = pAB[:, :].rearrange("p (s c) -> p s c", c=256)
            qa, ka, qb, kb = ba[0], ba[1], ba[2], ba[3]
            kT_for_tok = kb if is_hg else ka
            q_inter = qb if is_hg else qa
            NV = 128 if is_hg else 129
            chk("h%d_proj" % hd)
            qa, ka, qb, kb = ba[0], ba[1], ba[2], ba[3]
            if is_hg:
                q32, sig, lf, kin, Bc, Dd = fa
                c = h * 4
                f.op(act, lambda: Sx.activation(q32[:, :], p2[:, :], AF.Silu, bias=pp[:, c:c + 1]), reads=[p2, pp], writes=[q32])
                f.op(dve, lambda: V.tensor_copy(vtok[:, :, 0:128], pvg[:, :, 0:128]), reads=[pAB], writes=[vtok])
                f.op(act, lambda: Sx.activation(gw[:, :, :], pvg[:, :, 128:256], AF.Silu), reads=[pAB], writes=[gw])
                f.op(pool, lambda: G.tensor_tensor(gw[:, :, :], gw[:, :, :], hgw[:, h * 128:(h + 1) * 128].unsqueeze(1).to_broadcast([128, 4, 128]), ALU.mult),
                     reads=[gw, hgw], writes=[gw])
                f.op(act, lambda: Sx.activation(sig[:, :], p3[:, :], AF.Sigmoid, bias=pp[:, c + 1:c + 2]), reads=[p3, pp], writes=[sig])
                f.op(act, lambda: Sx.activation(lf[:, :], sig[:, :], AF.Ln, bias=pd[:, h * 3:h * 3 + 1], scale=pd[:, h * 3 + 1:h * 3 + 2]), reads=[sig, pd], writes=[lf])
                f.op(dve, lambda: V.tensor_scalar(kin[:, :], sig[:, :], pd[:, h * 3 + 2:h * 3 + 3], pd[:, h * 3 + 1:h * 3 + 2], ALU.mult, ALU.add), reads=[sig, pd], writes=[kin])
                f.op(dve, lambda: V.tensor_tensor_scan(Bc[:, :], ones[:, :], lf[:, :], 0.0, ALU.mult, ALU.add), reads=[ones, lf], writes=[Bc])
                Bv = Bc[:, :].rearrange("p (c t) -> p c t", t=64)
                f.op(dve, lambda: V.memset(sm2[:, 0:8], 0.0), writes=[sm2])
                f.op(dve, lambda: V.tensor_copy(sm2[:, 1:8], Bv[:, 0:7, 63]), reads=[Bc], writes=[sm2])
                f.op(dve, lambda: V.tensor_tensor(sm2[:, 8:16], Bv[:, :, 31], sm2[:, 0:8], ALU.subtract), reads=[Bc, sm2], writes=[sm2])
                f.op(dve, lambda: V.tensor_tensor(sm2[:, 16:24], Bv[:, :, 63], Bv[:, :, 31], ALU.subtract), reads=[Bc], writes=[sm2])
                f.op(dve, lambda: V.tensor_tensor(sm2[:, 24:32], Bv[:, :, 63], sm2[:, 0:8], ALU.subtract), reads=[Bc, sm2], writes=[sm2])
                f.op(act, lambda: Sx.activation(sm2[:, 8:32], sm2[:, 8:32], AF.Exp), reads=[sm2], writes=[sm2])
                Dv = Dd[:, :].rearrange("p (c t) -> p c t", t=64)
                f.op(dve, lambda: V.tensor_tensor(Dv, Bv, Bv[:, :, 31:32].to_broadcast([128, 8, 64]), ALU.subtract), reads=[Bc], writes=[Dd])
                E1, E1i = lf, sig
                f.op(act, lambda: Sx.activation(E1[:, :], Dd[:, :], AF.Exp), reads=[Dd], writes=[E1])
                f.op(act, lambda: Sx.activation(E1i[:, :], Dd[:, :], AF.Exp, scale=-1.0), reads=[Dd], writes=[E1i])
                f.op(dve, lambda: V.tensor_tensor(E1[:, :], q32[:, :], E1[:, :], ALU.mult), reads=[q32, E1], writes=[E1])
                f.op(dve, lambda: V.tensor_tensor(E1i[:, :], kin[:, :], E1i[:, :], ALU.mult), reads=[kin, E1i], writes=[E1i])
                f.op(pool, lambda: G.tensor_copy(qa[:, :], E1[:, :]), reads=[E1], writes=[qa])
                f.op(pool, lambda: G.tensor_copy(ka[:, :], E1i[:, :]), reads=[E1i], writes=[ka])
                f.op(dve, lambda: V.tensor_tensor(kin[:, :].rearrange("p (c t) -> p c t", t=64), E1i[:, :].rearrange("p (c t) -> p c t", t=64),
                                                  sm2[:, 16:24].unsqueeze(2).to_broadcast([128, 8, 64]), ALU.mult), reads=[E1i, sm2], writes=[kin])
                f.op(act, lambda: Sx.copy(kb[:, :], kin[:, :]), reads=[kin], writes=[kb])
                f.op(dve, lambda: V.tensor_tensor(q32[:, :].rearrange("p (c t) -> p c t", t=64), E1[:, :].rearrange("p (c t) -> p c t", t=64),
                                                  sm2[:, 8:16].unsqueeze(2).to_broadcast([128, 8, 64]), ALU.mult), reads=[E1, sm2], writes=[q32])
                f.op(act, lambda: Sx.copy(qb[:, :], q32[:, :]), reads=[q32], writes=[qb])
                kT_for_tok = kb
                q_inter = qb
                NV = 128
            else:
                cb0 = 16 + h * 12
                for s in range(4):
                    f.op(dve, lambda: V.tensor_scalar(vtok[:, s, 0:128], pvg[:, s, 0:128], eftok[:, s, 0, h:h + 1], None, ALU.mult), reads=[pAB, eftok], writes=[vtok])
                f.op(dve, lambda: V.tensor_copy(vtok[:, :, 128], eftok[:, :, 0, h]), reads=[eftok], writes=[vtok])
                f.op(act, lambda: Sx.activation(gw[:, :, :], pvg[:, :, 128:256], AF.Sigmoid), reads=[pAB], writes=[gw])
                f.op(pool, lambda: G.tensor_tensor(gw[:, :, :], gw[:, :, :], mlw[:, h * 128:(h + 1) * 128].unsqueeze(1).to_broadcast([128, 4, 128]), ALU.mult),
                     reads=[gw, mlw], writes=[gw])
                for blk, pb, dst in ((0, p2, qa), (1, p3, ka)):
                    hb = h * 2 + blk
                    ubm = ubs[blk]
                    ubmh = ubs_h[blk]
                    f.op(dve, lambda: V.tensor_copy(ubmh[:, 0:3], hml[hb][:, :]), reads=[hml[hb]], writes=[ubmh])
                    f.op(act, lambda: Sx.activation(ubm[:, 3:515], pb[:, :], AF.Identity, bias=pp[:, cb0 + blk:cb0 + blk + 1]), reads=[pb, pp], writes=[ubm])
                    f.op(pool, lambda: G.tensor_copy(hml[hb][:, :], ubm[:, 512:515]), reads=[ubm], writes=[hml[hb]])
                    acc = fa[blk]
                    wc = cb0 + 2 + blk * 4
                    f.op(dve, lambda: V.tensor_scalar(acc[:, :], ubm[:, 3:515], pp[:, wc + 3:wc + 4], pp[:, cb0 + 10 + blk:cb0 + 11 + blk], ALU.mult, ALU.add),
                         reads=[ubm, pp], writes=[acc])
                    for j in range(3):
                        f.op(dve, lambda: V.scalar_tensor_tensor(acc[:, :], ubm[:, j:j + 512], pp[:, wc + j:wc + j + 1], acc[:, :], ALU.mult, ALU.add),
                             reads=[ubm, ubmh, pp, acc], writes=[acc])
                    f.op(act, lambda: Sx.activation(acc[:, :], acc[:, :], AF.Silu), reads=[acc], writes=[acc])
                    if blk == 0:
                        f.op(dve, lambda: V.tensor_scalar(dst[:, :], acc[:, :], 128.0 ** -0.5, None, ALU.mult), reads=[acc], writes=[dst])
                    else:
                        f.op(pool, lambda: G.tensor_copy(dst[:, :], acc[:, :]), reads=[acc], writes=[dst])
                kT_for_tok = ka
                q_inter = qa
                NV = 129


        def partC(hd):
            is_hg = hd < 4
            h = hd % 4
            W = Whd[hd % 2]
            brow = browb[hd % 2]
            pvg = pAB[:, :].rearrange("p (s c) -> p s c", c=256)
            qa, ka, qb, kb = ba[0], ba[1], ba[2], ba[3]
            kT_for_tok = kb if is_hg else ka
            q_inter = qb if is_hg else qa
            NV = 128 if is_hg else 129
            chk("h%d_elem" % hd)
            p7v = p7b[:, 0:512].rearrange("p (s d) -> p s d", d=128)
            for s in range(4):
                f.op(pe, lambda: Tn.transpose(p7v[:, s, :], kT_for_tok[:, s * 128:(s + 1) * 128], identb[:, :]), reads=[kT_for_tok, identb], writes=[p7])
            f.op(act, lambda: Sx.copy(ktok[:, :, :], p7v), reads=[p7], writes=[ktok])
            chk("h%d_ktok" % hd)
            p4v = p4[:, :].rearrange("p (s t) -> p s t", t=128)
            for s in range(4):
                f.op(pe, lambda: Tn.matmul(p4v[:, s, :], ka[:, s * 128:(s + 1) * 128], qa[:, s * 128:(s + 1) * 128], start=True, stop=True),
                     reads=[ka, qa], writes=[p4])
            f.op(dve, lambda: V.tensor_tensor(attb[:, :, :], p4v, maskT.unsqueeze(1).to_broadcast([128, 4, 128]), ALU.mult), reads=[p4, cst], writes=[attb])
            chk("h%d_att" % hd)
            pbanks = [[p5, p4], [p6, p7]]
            for c8 in range(8):
                pbk = pbanks[c8 % 2][(c8 // 2) // 3]
                o = ((c8 // 2) % 3) * 132
                lo = (c8 % 2) * 64
                chk("h%d_P%d" % (hd, c8))
                f.op(pe, lambda: Tn.matmul(pbk[:, o:o + NV], ktok[lo:lo + 64, c8 // 2, :], vtok[lo:lo + 64, c8 // 2, 0:NV], start=True, stop=True),
                     reads=[ktok, vtok], writes=[pbk])
            chk("h%d_P" % hd)
            St = Shg[h] if is_hg else Cml[h]
            for c8 in range(8):
                pbk = pbanks[c8 % 2][(c8 // 2) // 3]
                o = ((c8 // 2) % 3) * 132
                if is_hg:
                    f.op(dve, lambda: V.tensor_copy(Sbf[:, c8, 0:NV], St[:, :]), reads=[St], writes=[Sbf])
                    dec = sm2[:, 24 + c8:25 + c8]
                else:
                    dec = decbc[:, h * 8 + c8:h * 8 + c8 + 1]
                    f.op(dve, lambda: V.tensor_scalar(Sbf[:, c8, 0:NV], St[:, :], dec, None, ALU.mult), reads=[St, decbc], writes=[Sbf])
                f.op(dve, lambda: V.scalar_tensor_tensor(St[:, :], St[:, :], dec, pbk[:, o:o + NV], ALU.mult, ALU.add), reads=[St, pbk, sm2, decbc], writes=[St])
            chk("h%d_chain" % hd)
            def pvT(s_):
                return p5 if s_ < 2 else p6

            def pv(s_, a_, b_):
                return pvT(s_)[:, (s_ % 2) * 256 + a_:(s_ % 2) * 256 + b_]
            p5v = p5[:, :].rearrange("p (s c) -> p s c", c=256)
            p6v = p6[:, :].rearrange("p (s c) -> p s c", c=256)
            f.op(pool, lambda: G.tensor_tensor(qpA[:, :], q_inter[:, :], mskA[:, :], ALU.mult), reads=[q_inter, mskA], writes=[qpA])
            f.op(pool, lambda: G.tensor_tensor(qpB[:, :], q_inter[:, :], mskB[:, :], ALU.mult), reads=[q_inter, mskB], writes=[qpB])
            for s in range(4):
                f.op(pe, lambda: Tn.matmul(pv(s, 0, NV), attb[:, s, :], vtok[:, s, 0:NV], start=True, stop=False), reads=[attb, vtok], writes=[pvT(s)])
                f.op(pe, lambda: Tn.matmul(pv(s, 0, NV), qpA[:, s * 128:s * 128 + 128], Sbf[:, 2 * s, 0:NV], start=False, stop=False),
                     reads=[qpA, Sbf], writes=[pvT(s)])
                f.op(pe, lambda: Tn.matmul(pv(s, 0, NV), qpB[:, s * 128:s * 128 + 128], Sbf[:, 2 * s + 1, 0:NV], start=False, stop=True),
                     reads=[qpB, Sbf], writes=[pvT(s)])
            chk("h%d_omm" % hd)
            if is_hg:
                for s in range(4):
                    f.op(act, lambda: Sx.activation(junk[:, :], pv(s, 0, 128), AF.Square, accum_out=sm[:, 20 + s:21 + s]), reads=[pvT(s)], writes=[junk, sm])
                f.op(dve, lambda: V.tensor_scalar(sm[:, 24:28], sm[:, 20:24], 1.0 / 128.0, EPS, ALU.mult, ALU.add), reads=[sm], writes=[sm])
                f.op(act, lambda: Sx.activation(sm[:, 24:28], sm[:, 24:28], AF.Ln), reads=[sm], writes=[sm])
                f.op(act, lambda: Sx.activation(sm[:, 24:28], sm[:, 24:28], AF.Exp, scale=-0.5), reads=[sm], writes=[sm])
                for s in range(4):
                    f.op(dve, lambda: V.scalar_tensor_tensor(ybf[:, s, :], pv(s, 0, 128), sm[:, 24 + s:25 + s], gw[:, s, :], ALU.mult, ALU.mult),
                         reads=[pvT(s), sm, gw], writes=[ybf])
            else:
                f.op(act, lambda: Sx.activation(sm[:, 20:22], p5v[:, :, 128], AF.Abs), reads=[p5], writes=[sm])
                f.op(act, lambda: Sx.activation(sm[:, 22:24], p6v[:, :, 128], AF.Abs), reads=[p6], writes=[sm])
                f.op(dve, lambda: V.tensor_tensor(sm[:, 20:24], sm[:, 20:24], eftok[:, :, 1, h], ALU.max), reads=[sm, eftok], writes=[sm])
                f.op(dve, lambda: V.reciprocal(sm[:, 24:28], sm[:, 20:24]), reads=[sm], writes=[sm])
                f.op(dve, lambda: V.tensor_tensor(h32[:, 0:2, :], p5v[:, :, 0:128], sm[:, 24:26].unsqueeze(2).to_broadcast([128, 2, 128]), ALU.mult),
                     reads=[p5, sm], writes=[h32])
                f.op(dve, lambda: V.tensor_tensor(h32[:, 2:4, :], p6v[:, :, 0:128], sm[:, 26:28].unsqueeze(2).to_broadcast([128, 2, 128]), ALU.mult),
                     reads=[p6, sm], writes=[h32])
                stv = sm[:, 28:52].rearrange("p (s k) -> p s k", k=6)
                for s in range(4):
                    f.op(dve, lambda: V.bn_stats(stv[:, s, :], h32[:, s, :]), reads=[h32], writes=[sm])
                    f.op(dve, lambda: V.bn_aggr(sm[:, 52 + 2 * s:54 + 2 * s], stv[:, s, :]), reads=[sm], writes=[sm])
                mvv = sm[:, 52:60].rearrange("p (s k) -> p s k", k=2)
                f.op(dve, lambda: V.tensor_scalar(sm[:, 60:64], mvv[:, :, 1], EPS, None, ALU.add), reads=[sm], writes=[sm])
                f.op(act, lambda: Sx.activation(sm[:, 60:64], sm[:, 60:64], AF.Ln), reads=[sm], writes=[sm])
                f.op(act, lambda: Sx.activation(sm[:, 60:64], sm[:, 60:64], AF.Exp, scale=-0.5), reads=[sm], writes=[sm])
                f.op(dve, lambda: V.tensor_tensor(gw[:, :, :], gw[:, :, :], sm[:, 60:64].unsqueeze(2).to_broadcast([128, 4, 128]), ALU.mult), reads=[gw, sm], writes=[gw])
                for s in range(4):
                    f.op(dve, lambda: V.scalar_tensor_tensor(ybf[:, s, :], h32[:, s, :], mvv[:, s, 0:1], gw[:, s, :], ALU.subtract, ALU.mult),
                         reads=[h32, sm, gw], writes=[ybf])
            chk("head%d_pre" % hd)
            p7w = p7b[:, 512:1024].rearrange("p (s d) -> p s d", d=128)
            for s in range(4):
                f.op(pe, lambda: Tn.transpose(p7w[:, s, :], ybf[:, s, :], identb[:, :]), reads=[ybf, identb], writes=[p7])
            f.op(act, lambda: Sx.copy(yT[:, hd, :], p7b[:, 512:1024]), reads=[p7], writes=[yT])
            chk("head%d_post" % hd)

        partA(0)
        partB(0)
        for hd in range(8):
            if hd + 1 < 8:
                partA(hd + 1)
            partC(hd)
            if hd + 1 < 8:
                partB(hd + 1)

        chk("mixer")
        if "yT" in dumps and it == ntiles - 1:
            dump("yT", yT, yT[:, :, :], [128, 8, 512])
        f.dma(sp, wsqb[1], wsqb[1][:, :, :], wsq_st, wsq_s[1].rearrange("p (k c) -> p k c", c=1024))
        proj_res_ln(wsqb[0], yT, 0)
        if "x1" in dumps and it == ntiles - 1:
            pass
        transposes_to_xT(4, None)
        chk("ln1")
        for cbk in range(8):
            pb = p2 if cbk % 2 == 0 else p3
            for kc in range(8):
                f.op(pe, lambda: Tn.matmul(pb[:, :], wsqb[1][:, kc, cbk * 128:(cbk + 1) * 128], xT[:, kc, :], start=(kc == 0), stop=(kc == 7)),
                     reads=[wsqb[1], xT], writes=[pb])
            if cbk % 2 == 0:
                f.op(dve, lambda: V.tensor_copy(qTc[:, cbk, :], pb[:, :]), reads=[pb], writes=[qTc])
            else:
                f.op(act, lambda: Sx.copy(qTc[:, cbk, :], pb[:, :]), reads=[pb], writes=[qTc])
        f.dma(sp, wsqb[0], wsqb[0][:, :, :], wsq_st, wsq_s[2].rearrange("p (k c) -> p k c", c=1024))
        for h in range(4):
            for mc in range(2):
                pb = p2 if mc == 0 else p3
                for j in range(2):
                    f.op(pe, lambda: Tn.matmul(pb[:, :], KT[:, 2 * h + j, mc * 128:(mc + 1) * 128], qTc[:, 2 * h + j, :], start=(j == 0), stop=(j == 1)),
                         reads=[KT, qTc], writes=[pb])
                f.op(act, lambda: Sx.activation(ETb[mc][:, :], pb[:, :], AF.Exp, scale=1.0 / 16.0), reads=[pb], writes=[ETb[mc]])
            for mc in range(2):
                f.op(pe, lambda: Tn.matmul(p4[:, :], onesb[:, :], ETb[mc][:, :], start=(mc == 0), stop=(mc == 1)), reads=[onesb, ETb[mc]], writes=[p4])
            rden = fa[0]
            f.op(dve, lambda: V.reciprocal(rden[:, :], p4[:, :]), reads=[p4], writes=[rden])
            for j in range(2):
                pb = p5 if j == 0 else p6
                for mc in range(2):
                    f.op(pe, lambda: Tn.matmul(pb[:, :], Vm[:, mc, (2 * h + j) * 128:(2 * h + j + 1) * 128], ETb[mc][:, :], start=(mc == 0), stop=(mc == 1)),
                         reads=[Vm, ETb[mc]], writes=[pb])
                f.op(dve, lambda: V.tensor_tensor(yT[:, 2 * h + j, :], pb[:, :], rden[:, :], ALU.mult), reads=[pb, rden], writes=[yT])
        proj_res_ln(wsqb[0], yT, 1)
        if "x2" in dumps and it == ntiles - 1:
            pass
        transposes_to_xT(4, None)
        chk("ca")
        pdn = [pAB, pAB, p5, p6]
        def pdn_ap(s):
            return pAB[:, s * 512:(s + 1) * 512] if s < 2 else pdn[s][:, :]
        def ffn_load(g):
            f.dma(sp, wupb[g % 2], wupb[g % 2][:, :, :, :], wup_st, wup_s[g].rearrange("p (j k c) -> p j k c", j=2, k=8))
            f.dma(sp, wdnb[g % 3], wdnb[g % 3][:, :, :], wdn_st, wdn_s[0, g].rearrange("p (j c) -> p j c", j=2))

        def ffn_up(j):
            g, jj = j // 2, j % 2
            if jj == 0 and g + 1 < 11:
                ffn_load(g + 1)
            wu = wupb[g % 2]
            for gv in range(2):
                pb = pup[(2 * j + gv) % 4]
                for kc in range(8):
                    f.op(pe, lambda: Tn.matmul(pb[:, :], wu[:, jj, kc, gv * 128:(gv + 1) * 128], xT[:, kc, :], start=(kc == 0), stop=(kc == 7)),
                         reads=[wu, xT], writes=[pb])

        def ffn_ew(j):
            accs = []
            for gv in range(2):
                pb = pup[(2 * j + gv) % 4]
                ub = ubs[(j % 2) * 2 + gv]
                bidx = j + 22 * gv
                ubh = ubs_h[(j % 2) * 2 + gv]
                f.op(dve, lambda: V.tensor_copy(ubh[:, 0:2], hff[bidx][:, :]), reads=[hff[bidx]], writes=[ubh])
                f.op(act, lambda: Sx.copy(ub[:, 2:514], pb[:, :]), reads=[pb], writes=[ub])
                f.op(pool, lambda: G.tensor_copy(hff[bidx][:, :], ub[:, 512:514]), reads=[ub], writes=[hff[bidx]])
                acc = fa[(j % 2) * 2 + gv]
                pc = 64 + bidx * 4
                f.op(act, lambda: Sx.activation(acc[:, :], pb[:, :], AF.Identity, bias=pp[:, pc + 3:pc + 4], scale=pp[:, pc + 2:pc + 3]), reads=[pb, pp], writes=[acc])
                for t in range(2):
                    f.op(dve, lambda: V.scalar_tensor_tensor(acc[:, :], ub[:, t:t + 512], pp[:, pc + t:pc + t + 1], acc[:, :], ALU.mult, ALU.add),
                         reads=[ub, ubh, pp, acc], writes=[acc])
                accs.append(acc)
            f.op(act, lambda: Sx.activation(accs[0][:, :], accs[0][:, :], AF.Gelu_apprx_tanh), reads=[accs[0]], writes=[accs[0]])
            f.op(dve, lambda: V.tensor_tensor(hTj[j][:, :], accs[0][:, :], accs[1][:, :], ALU.mult), reads=accs, writes=[hTj[j]])

        def ffn_down(j):
            g, jj = j // 2, j % 2
            wd = wdnb[g % 3]
            for s in range(4):
                f.op(pe, lambda: Tn.matmul(pdn_ap(s), hTj[j][:, s * 128:(s + 1) * 128], wd[:, jj, :], start=(j == 0), stop=(j == 21)),
                     reads=[hTj[j], wd], writes=[pdn[s]])

        ffn_load(0)
        ffn_up(0)
        ffn_up(1)
        for j in range(22):
            ffn_ew(j)
            if j + 2 < 22:
                ffn_up(j + 2)
            ffn_down(j)
        f.dma(sp, lnp, lnp[:, :, :], None,
              rows_d[0:1, R_LN + 2 * 2048:R_LN + 3 * 2048].rearrange("o (a d) -> o a d", a=2).partition_broadcast(128))
        for s in range(4):
            f.op(dve, lambda: V.scalar_tensor_tensor(xr[s][:, 0:512], xr[s][:, 0:512], ALPHA, pdn_ap(s), ALU.mult, ALU.add), reads=[xr[s], pdn[s]], writes=[xr[s]])
        for g0 in range(2):
            f.dma(sp, wdnb[g0 % 3], wdnb[g0 % 3][:, :, :], wdn_st, wdn_s[1, g0].rearrange("p (j c) -> p j c", j=2))
        for g in range(11):
            if g + 2 < 11:
                f.dma(sp, wdnb[(g + 2) % 3], wdnb[(g + 2) % 3][:, :, :], wdn_st, wdn_s[1, g + 2].rearrange("p (j c) -> p j c", j=2))
            wd = wdnb[g % 3]
            for jj in range(2):
                j = 2 * g + jj
                for s in range(4):
                    f.op(pe, lambda: Tn.matmul(pdn_ap(s), hTj[j][:, s * 128:(s + 1) * 128], wd[:, jj, :], start=(j == 0), stop=(j == 21)),
                         reads=[hTj[j], wd], writes=[pdn[s]])
        for s in range(4):
            f.op(dve, lambda: V.scalar_tensor_tensor(xr[s][:, 512:1024], xr[s][:, 512:1024], ALPHA, pdn_ap(s), ALU.mult, ALU.add), reads=[xr[s], pdn[s]], writes=[xr[s]])
            layernorm_rows(s, 2)
            if s > 0:
                layernorm_apply(s - 1)
                f.dma(sp, out_t, out_d[t0 + (s - 1) * 128:t0 + s * 128, :], xr[s - 1], xr[s - 1][:, :])
        layernorm_apply(3)
        f.dma(sp, out_t, out_d[t0 + 3 * 128:t0 + 4 * 128, :], xr[3], xr[3][:, :])


def host_prep(inp):
    w_in = inp["w_in"][0]
    b_in = inp["b_in"][0]

    def tile_k(w):
        return w.reshape(8, 128, -1).transpose(1, 0, 2)
    whd = np.empty((8, 128, 8, 512), np.float32)
    brow = np.empty((8, 256), np.float32)
    for h in range(4):
        cols = [slice(0 + h * 128, 128 + h * 128), slice(512 + h * 128, 640 + h * 128), slice(1024 + h * 128, 1152 + h * 128), slice(1536 + h * 128, 1664 + h * 128)]
        whd[h] = np.concatenate([tile_k(w_in[:, c]) for c in cols], axis=2)
        brow[h] = np.concatenate([b_in[cols[2]], b_in[cols[3]]])
        cols = [slice(2048 + h * 128, 2176 + h * 128), slice(2560 + h * 128, 2688 + h * 128), slice(3072 + h * 128, 3200 + h * 128), slice(3584 + h * 128, 3712 + h * 128)]
        whd[4 + h] = np.concatenate([tile_k(w_in[:, c]) for c in cols], axis=2)
        brow[4 + h] = np.concatenate([b_in[cols[2]], b_in[cols[3]]])
    wg = tile_k(w_in[:, 4096:4104]).reshape(128, 64)
    wsq = np.stack([tile_k(inp["w_out"][0]), tile_k(inp["ca_wq"][0]), tile_k(inp["ca_wo"][0])]).reshape(3, 128, 8192)
    wkv = inp["ca_wkv"][0]
    wkv_t = np.stack([tile_k(wkv[:, :1024]), tile_k(wkv[:, 1024:])]).reshape(2, 128, 8192)
    wu = inp["ffn_w_up"][0]
    wup = np.empty((11, 128, 2, 8, 256), np.float32)
    for j in range(22):
        blk = np.concatenate([tile_k(wu[:, j * 128:(j + 1) * 128]), tile_k(wu[:, 2816 + j * 128:2816 + (j + 1) * 128])], axis=2)
        wup[j // 2, :, j % 2] = blk
    wd = inp["ffn_w_down"][0]
    wdn = np.empty((2, 11, 128, 2, 512), np.float32)
    for j in range(22):
        for hf in range(2):
            wdn[hf, j // 2, :, j % 2] = wd[j * 128:(j + 1) * 128, hf * 512:(hf + 1) * 512]
    pp = np.zeros((128, NPP), np.float32)
    lbl = inp["hg_lb_logits"]
    cw = inp["ml_conv_w"][0]; cbias = inp["ml_conv_b"][0]
    for h in range(4):
        sl = slice(h * 128, (h + 1) * 128)
        pp[:, h * 4 + 0] = b_in[0 + h * 128:128 + h * 128]
        pp[:, h * 4 + 1] = b_in[512 + h * 128:640 + h * 128]
        pp[:, h * 4 + 2] = lbl[0, sl]
        pp[:, h * 4 + 3] = lbl[1, sl]
        c0 = 16 + h * 12
        pp[:, c0 + 0] = b_in[2048 + h * 128:2176 + h * 128]
        pp[:, c0 + 1] = b_in[2560 + h * 128:2688 + h * 128]
        for blk in range(2):
            csl = slice(blk * 512 + h * 128, blk * 512 + (h + 1) * 128)
            for j in range(4):
                pp[:, c0 + 2 + blk * 4 + j] = cw[j, csl]
            pp[:, c0 + 10 + blk] = cbias[csl]
    fw = inp["ffn_conv_w"][0]; fb = inp["ffn_conv_b"][0]
    for b in range(44):
        sl = slice(b * 128, (b + 1) * 128)
        for j in range(3):
            pp[:, 64 + b * 4 + j] = fw[j, sl]
        pp[:, 64 + b * 4 + 3] = fb[sl]
    pp[0:4, 240] = b_in[4096:4100]
    pp[0:4, 241] = b_in[4100:4104]
    rows = np.concatenate([inp["hg_norm_w"][0], inp["ml_norm_w"][0], inp["ln1_g"][0], inp["ln1_b"][0], inp["ln2_g"][0], inp["ln2_b"][0],
                           inp["ln3_g"][0], inp["ln3_b"][0], brow.reshape(-1)]).astype(np.float32)[None, :]
    cst = np.zeros((128, 512), np.float32)
    cst[:, 0:128] = np.eye(128, dtype=np.float32)
    idx = np.arange(128)
    cst[:, 128:256] = ((idx[:, None] // 64 == idx[None, :] // 64) & (idx[:, None] <= idx[None, :])).astype(np.float32)
    for k in range(4):
        cst[k, 256 + k * 8:256 + (k + 1) * 8] = 1.0
    shared = dict(whd=np.ascontiguousarray(whd.reshape(8, 128, 4096)), wg=np.ascontiguousarray(wg), wsq=np.ascontiguousarray(wsq),
                  wkv=np.ascontiguousarray(wkv_t), wup=np.ascontiguousarray(wup.reshape(11, 128, 4096)),
                  wdn=np.ascontiguousarray(wdn.reshape(2, 11, 128, 1024)), pp=pp, rows=np.ascontiguousarray(rows), cst=cst)
    return shared


_NC_CACHE = {}


def kernel(**inputs):
    inp = {k: np.asarray(v) for k, v in inputs.items()}
    shared = host_prep(inp)
    if "nc" not in _NC_CACHE:
        _NC_CACHE["nc"] = build()
    nc = _NC_CACHE["nc"]
    in_maps = []
    for b in range(8):
        m = dict(shared)
        m["x"] = np.ascontiguousarray(inp["x"][b])
        m["mem"] = np.ascontiguousarray(inp["mem"][b])
        in_maps.append(m)
    res = run_bass_kernel_spmd(nc, in_maps, core_ids=list(range(8)))
    return np.stack([np.asarray(r["out"]) for r in res.results]).astype(np.float32)
```

```python
import numpy as np
import concourse.bass as bass
import concourse.mybir as mybir
from concourse.bass_utils import run_bass_kernel_spmd

F32 = mybir.dt.float32
BF16 = mybir.dt.bfloat16
AF = mybir.ActivationFunctionType
ALU = mybir.AluOpType
AX = mybir.AxisListType

S = 4096
D = 1024
TM = 512
NTILES = S // TM
ALPHA = 2.0 ** 0.25
EPS = 1e-5
NPP = 256
R_HGW, R_MLW, R_LN, R_B = 0, 512, 1024, 1024 + 6 * 1024
NR = R_B + 2048


class Eng:
    def __init__(self, name, eng, sem, inc=1):
        self.name, self.eng, self.sem, self.inc = name, eng, sem, inc
        self.count = 0
        self.waited = {}


class T:
    def __init__(self, h, name=None):
        self.h = h
        self.name = name
        self.w = None
        self.r = {}
        self.dma = None
        self.psum = False
        self.group = [self]

    def __getitem__(self, k):
        return self.h[k]


class TV:
    def __init__(self, parent, ap):
        self.__dict__["p"] = parent
        self.__dict__["h"] = ap

    def __getitem__(self, k):
        return self.h[k]

    def __getattr__(self, k):
        return getattr(self.__dict__["p"], k)

    def __setattr__(self, k, v):
        setattr(self.__dict__["p"], k, v)


def alias(*ts):
    g = []
    for t in ts:
        for u in t.group:
            if u not in g:
                g.append(u)
    for t in g:
        t.group = g


class FW:
    def __init__(self, nc):
        self.nc = nc
        self.pe = self._mk("pe", nc.tensor)
        self.act = self._mk("act", nc.scalar)
        self.dve = self._mk("dve", nc.vector)
        self.pool = self._mk("pool", nc.gpsimd)
        self.sp = self._mk("sp", nc.sync)
        self.ndma = 0

    def _mk(self, name, eng, inc=1):
        sem = self.nc.semaphore(name).__enter__()
        return Eng(name, eng, sem, inc)

    def sb(self, name, shape, dt):
        return T(self.nc.alloc_sbuf_tensor("sb_" + name, list(shape), dt), name)

    def ps(self, name, shape, dt=F32):
        t = T(self.nc.alloc_psum_tensor("ps_" + name, list(shape), dt), name)
        t.psum = True
        return t

    def _deps(self, reads, writes, E=None):
        deps = {}

        def add(e, c):
            if deps.get(e, 0) < c:
                deps[e] = c
        for t0 in reads:
            for t in t0.group:
                if t.w:
                    add(*t.w)
                if t.psum:
                    for e, c in t.r.items():
                        if e is not E:
                            add(e, c)
        for t0 in writes:
            for t in t0.group:
                if t.w:
                    add(*t.w)
                for e, c in t.r.items():
                    add(e, c)
        return deps

    def _wait(self, E, deps, skip_self=False):
        for e, c in deps.items():
            if e is E and skip_self:
                continue
            if E.waited.get(e, 0) < c:
                E.eng.wait_ge(e.sem, c * e.inc)
                E.waited[e] = c

    def op(self, E, fn, reads=(), writes=()):
        deps = self._deps(reads, writes, E)
        self._wait(E, deps, skip_self=(E is self.pe))
        inst = fn()
        E.count += 1
        inst.then_inc(E.sem, 1)
        for t in reads:
            if t.r.get(E, 0) < E.count:
                t.r[E] = E.count
        for t in writes:
            t.w = (E, E.count)
            t.r = {}
        return inst

    def dma(self, E, out_t, out_ap, in_t, in_ap, **kw):
        tgt = out_t if out_t is not None else in_t
        if tgt.dma is None:
            tgt.dma = self._mk("dma%d" % self.ndma, None, inc=16)
            self.ndma += 1
        Dq = tgt.dma
        reads = [in_t] if in_t is not None else []
        writes = [out_t] if out_t is not None else []
        deps = self._deps(reads, writes)
        self._wait(E, deps)
        inst = E.eng.dma_start(out=out_ap, in_=in_ap, **kw)
        Dq.count += 1
        inst.then_inc(Dq.sem, 16)
        for t in reads:
            if t.r.get(Dq, 0) < Dq.count:
                t.r[Dq] = Dq.count
        for t in writes:
            t.w = (Dq, Dq.count)
            t.r = {}
        return inst

    def finish(self, E, ts):
        deps = {}
        for t in ts:
            if t.w and deps.get(t.w[0], 0) < t.w[1]:
                deps[t.w[0]] = t.w[1]
        for e, c in deps.items():
            E.eng.wait_ge(e.sem, c * e.inc)


class _Stop(Exception):
    pass


def build(ntiles=NTILES, dumps=(), stop=None):
    nc = bass.Bass("TRN2", target_bir_lowering=False)
    f = FW(nc)
    try:
        _build_body(nc, f, ntiles, dumps, stop)
    except _Stop:
        pass
    f.finish(f.sp, f.final_ts)
    return nc


def _build_body(nc, f, ntiles, dumps, stop):
    def chk(tag):
        if stop == tag:
            raise _Stop()
    pe, act, dve, pool, sp = f.pe, f.act, f.dve, f.pool, f.sp
    V, Sx, Tn, G = nc.vector, nc.scalar, nc.tensor, nc.gpsimd

    def din(name, shape, dt=F32):
        return nc.dram_tensor(name, list(shape), dt, kind="ExternalInput").ap()

    x_d = din("x", [S, D])
    mem_d = din("mem", [256, D])
    whd_d = din("whd", [8, 128, 4096])
    wg_d = din("wg", [128, 64])
    wsq_d = din("wsq", [3, 128, 8192])
    wkv_d = din("wkv", [2, 128, 8192])
    wup_d = din("wup", [11, 128, 4096])
    wdn_d = din("wdn", [2, 11, 128, 1024])
    pp_d = din("pp", [128, NPP])
    rows_d = din("rows", [1, NR])
    cst_d = din("cst", [128, 512])
    out_d = nc.dram_tensor("out", [S, D], F32, kind="ExternalOutput").ap()
    out_t = T(None, "out")
    dump_ts = [out_t]
    f.final_ts = dump_ts

    def scratch(name, shape):
        return nc.dram_tensor(name, list(shape), BF16, kind="Internal").ap()
    whd_s = scratch("whd_s", [8, 128, 4096]); whd_st = T(None, "whd_s")
    wsq_s = scratch("wsq_s", [3, 128, 8192]); wsq_st = T(None, "wsq_s")
    wkv_s = scratch("wkv_s", [2, 128, 8192]); wkv_st = T(None, "wkv_s")
    wup_s = scratch("wup_s", [11, 128, 4096]); wup_st = T(None, "wup_s")
    wdn_s = scratch("wdn_s", [2, 11, 128, 1024]); wdn_st = T(None, "wdn_s")

    whd_sts = [T(None, "whd_s%d" % h) for h in range(8)]
    for i in range(2):
        f.dma(pool, wkv_st, wkv_s[i], None, wkv_d[i])
    for h in range(8):
        f.dma(pool, whd_sts[h], whd_s[h], None, whd_d[h])
    for i in range(3):
        f.dma(pool, wsq_st, wsq_s[i], None, wsq_d[i])
    for g in range(11):
        f.dma(pool, wup_st, wup_s[g], None, wup_d[g])
        for hf in range(2):
            f.dma(pool, wdn_st, wdn_s[hf, g], None, wdn_d[hf, g])

    chk("prologue")
    cst = f.sb("cst", [128, 512], F32)
    f.dma(sp, cst, cst[:, :], None, cst_d)
    ident = cst[:, 0:128]
    maskT = cst[:, 128:256]
    selm = cst[0:4, 256:288]
    identb = f.sb("identb", [128, 128], BF16)
    f.dma(pool, identb, identb[:, :], None, cst_d[:, 0:128])
    pp = f.sb("pp", [128, NPP], F32)
    f.dma(sp, pp, pp[:, :], None, pp_d)
    wg = f.sb("wg", [128, 8, 8], BF16)
    f.dma(pool, wg, wg[:, :, :], None, wg_d.rearrange("p (k c) -> p k c", c=8))
    ones = f.sb("ones", [128, 512], F32)
    f.op(pool, lambda: G.memset(ones[:, :], 1.0), writes=[ones])
    onesb = f.sb("onesb", [128, 128], BF16)
    f.op(pool, lambda: G.memset(onesb[:, :], 1.0), writes=[onesb])
    hgw = f.sb("hgw", [128, 512], F32)
    f.dma(sp, hgw, hgw[:, :], None, rows_d[0:1, R_HGW:R_HGW + 512].partition_broadcast(128))
    mlw = f.sb("mlw", [128, 512], F32)
    f.dma(sp, mlw, mlw[:, :], None, rows_d[0:1, R_MLW:R_MLW + 512].partition_broadcast(128))
    browb = [f.sb("brow%d" % i, [1, 256], F32) for i in range(2)]
    btmp = f.sb("btmp", [1, 256], F32)
    bhl_all = f.sb("bhl_all", [1, 8, 2, 256], BF16)
    for hd_ in range(8):
        brow_ = browb[hd_ % 2]
        f.dma(sp, brow_, brow_[:, :], None, rows_d[0:1, R_B + hd_ * 256:R_B + (hd_ + 1) * 256])
        f.op(dve, lambda: nc.vector.tensor_copy(bhl_all[:, hd_, 0, :], brow_[:, :]), reads=[brow_], writes=[bhl_all])
        f.op(dve, lambda: nc.vector.tensor_copy(btmp[:, :], bhl_all[:, hd_, 0, :]), reads=[bhl_all], writes=[btmp])
        f.op(dve, lambda: nc.vector.tensor_tensor(bhl_all[:, hd_, 1, :], brow_[:, :], btmp[:, :], ALU.subtract), reads=[brow_, btmp], writes=[bhl_all])
    lnp = f.sb("lnp", [128, 2, 1024], F32)

    pd = f.sb("pd", [128, 16], F32)
    for h in range(4):
        c = h * 4
        f.op(dve, lambda: V.tensor_tensor(pd[:, 13:14], pp[:, c + 2:c + 3], pp[:, c + 3:c + 4], ALU.subtract), reads=[pp], writes=[pd])
        f.op(act, lambda: Sx.activation(pd[:, h * 3:h * 3 + 1], pd[:, 13:14], AF.Sigmoid), reads=[pd], writes=[pd])
        f.op(dve, lambda: V.tensor_scalar(pd[:, h * 3 + 1:h * 3 + 2], pd[:, h * 3:h * 3 + 1], -1.0, 1.0, ALU.mult, ALU.add), reads=[pd], writes=[pd])
        f.op(dve, lambda: V.tensor_scalar(pd[:, h * 3 + 2:h * 3 + 3], pd[:, h * 3:h * 3 + 1], 1.0, -1.0, ALU.mult, ALU.add), reads=[pd], writes=[pd])
    f.op(dve, lambda: V.tensor_scalar(pd[:, 12:13], pp[:, 241:242], -1.0, None, ALU.mult), reads=[pp], writes=[pd])

    chk("consts")
    pAB = f.ps("pAB", [128, 1024])
    p2 = f.ps("p2", [128, 512]); p3 = f.ps("p3", [128, 512]); p4 = f.ps("p4", [128, 512])
    p5 = f.ps("p5", [128, 512]); p6 = f.ps("p6", [128, 512])
    p7 = f.ps("p7", [128, 512])
    p7b = TV(p7, p7.h.bitcast(BF16))

    Shg = [f.sb("Shg%d" % h, [128, 128], F32) for h in range(4)]
    Cml = [f.sb("Cml%d" % h, [128, 129], F32) for h in range(4)]
    for h in range(4):
        f.op(pool, lambda: G.memset(Shg[h][:, :], 0.0), writes=[Shg[h]])
        f.op(pool, lambda: G.memset(Cml[h][:, :], 0.0), writes=[Cml[h]])
    mcar = f.sb("mcar", [4, 1], F32)
    f.op(pool, lambda: G.memset(mcar[:, :], -1e30), writes=[mcar])
    halo_ml = f.sb("halo_ml", [128, 8, 3], F32)
    f.op(pool, lambda: G.memset(halo_ml[:, :, :], 0.0), writes=[halo_ml])
    halo_ff = f.sb("halo_ff", [128, 44, 2], F32)
    f.op(pool, lambda: G.memset(halo_ff[:, :, :], 0.0), writes=[halo_ff])

    KT = f.sb("KT", [128, 8, 256], BF16)
    Vm = f.sb("Vm", [128, 2, 1024], BF16)
    xres = f.sb("xres", [128, 4, 1024], F32)
    xr = [T(xres.h[:, s_, :], "xr%d" % s_) for s_ in range(4)]
    smln = [f.sb("smln%d" % s_, [128, 32], F32) for s_ in range(4)]
    xT = f.sb("xT", [128, 8, 512], BF16)
    wsqb = [f.sb("wsq%d" % i, [128, 8, 1024], BF16) for i in range(2)]

    def transposes_to_xT(nsub, src):
        for s in range(nsub):
            if s % 2 == 0:
                for kc in range(8):
                    f.op(pe, lambda: Tn.transpose(pAB[:, kc * 128:(kc + 1) * 128], xr[s][:, kc * 128:(kc + 1) * 128], ident),
                         reads=[xr[s], cst], writes=[pAB])
                f.op(dve, lambda: V.tensor_copy(xT[:, :, s * 128:(s + 1) * 128], pAB[:, :].rearrange("p (k t) -> p k t", t=128)),
                     reads=[pAB], writes=[xT])
            else:
                for kc in range(8):
                    pb = p5 if kc < 4 else p6
                    f.op(pe, lambda: Tn.transpose(pb[:, (kc % 4) * 128:(kc % 4 + 1) * 128], xr[s][:, kc * 128:(kc + 1) * 128], ident),
                         reads=[xr[s], cst], writes=[pb])
                f.op(act, lambda: Sx.copy(xT[:, 0:4, s * 128:(s + 1) * 128], p5[:, :].rearrange("p (k t) -> p k t", t=128)),
                     reads=[p5], writes=[xT])
                f.op(act, lambda: Sx.copy(xT[:, 4:8, s * 128:(s + 1) * 128], p6[:, :].rearrange("p (k t) -> p k t", t=128)),
                     reads=[p6], writes=[xT])

    for s_ in range(2):
        f.dma(sp, xr[s_], xr[s_][:, :], None, mem_d[s_ * 128:(s_ + 1) * 128, :])
    transposes_to_xT(2, None)
    for i in range(2):
        f.dma(sp, wsqb[i], wsqb[i][:, :, :], wkv_st, wkv_s[i].rearrange("p (k c) -> p k c", c=1024))
    for cb in range(8):
        for kc in range(8):
            f.op(pe, lambda: Tn.matmul(p2[:, 0:256], wsqb[0][:, kc, cb * 128:(cb + 1) * 128], xT[:, kc, 0:256], start=(kc == 0), stop=(kc == 7)),
                 reads=[wsqb[0], xT], writes=[p2])
        f.op(dve, lambda: V.tensor_copy(KT[:, cb, :], p2[:, 0:256]), reads=[p2], writes=[KT])
    for mc in range(2):
        for hf in range(2):
            for kc in range(8):
                f.op(pe, lambda: Tn.matmul(p3[:, :], xT[:, kc, mc * 128:(mc + 1) * 128], wsqb[1][:, kc, hf * 512:(hf + 1) * 512], start=(kc == 0), stop=(kc == 7)),
                     reads=[wsqb[1], xT], writes=[p3])
            f.op(act, lambda: Sx.copy(Vm[:, mc, hf * 512:(hf + 1) * 512], p3[:, :]), reads=[p3], writes=[Vm])

    chk("kv")
    Whd = [f.sb("Whd%d" % i, [128, 8, 512], BF16) for i in range(2)]
    yT = f.sb("yT", [128, 8, 512], BF16)
    fa = [f.sb("fa%d" % i, [128, 512], F32) for i in range(6)]
    ubuf = f.sb("ubuf", [128, 520], F32)
    ba = [f.sb("ba%d" % i, [128, 512], BF16) for i in range(5)]
    attb = f.sb("attb", [128, 4, 128], BF16)
    ktok = f.sb("ktok", [128, 4, 128], BF16)
    vtok = f.sb("vtok", [128, 4, 132], BF16)
    gw = f.sb("gw", [128, 4, 128], F32)
    ybf = f.sb("ybf", [128, 4, 128], BF16)
    Sbf = f.sb("Sbf", [128, 8, 132], BF16)
    sm = f.sb("sm", [128, 64], F32)
    sm2 = f.sb("sm2", [128, 64], F32)
    junk = f.sb("junk", [128, 128], F32)
    g_s = f.sb("g_s", [4, 64], F32)
    eftok = f.sb("eftok", [128, 4, 2, 4], F32)
    decbc = f.sb("decbc", [128, 32], F32)
    ddb = f.sb("ddb", [4, 2, 32], BF16)
    hT = f.sb("hT", [128, 22, 512], BF16)
    wupb = [f.sb("wup%d" % i, [128, 2, 8, 256], BF16) for i in range(2)]
    wdnb = [f.sb("wdn%d" % i, [128, 2, 512], BF16) for i in range(3)]
    ub2 = f.sb("ub2", [128, 520], F32)
    hTj = [T(hT.h[:, j, :], "hT%d" % j) for j in range(22)]
    qTc = T(hT.h[:, 0:8, :], "qTc")
    qTc.group = [qTc] + hTj[0:8]
    for j in range(8):
        hTj[j].group = [hTj[j], qTc]
    ubs = [ubuf, ub2, f.sb("ub3", [128, 520], F32), f.sb("ub4", [128, 520], F32)]
    ubs_h = [T(u.h[:, 0:8], "ubh") for u in ubs]
    hff = [T(halo_ff.h[:, b, :], "hff%d" % b) for b in range(44)]
    hml = [T(halo_ml.h[:, b, :], "hml%d" % b) for b in range(8)]
    for t_ in hff:
        t_.w = halo_ff.w
    for t_ in hml:
        t_.w = halo_ml.w
    pup = [p2, p3, p4, p7]
    qpA = f.sb("qpA", [128, 512], BF16)
    qpB = f.sb("qpB", [128, 512], BF16)
    mskA = f.sb("mskA", [128, 512], BF16)
    mskB = f.sb("mskB", [128, 512], BF16)
    f.op(pool, lambda: G.memset(mskA[:, :], 0.0), writes=[mskA])
    f.op(pool, lambda: G.memset(mskB[:, :], 0.0), writes=[mskB])
    for s in range(4):
        f.op(pool, lambda: G.memset(mskA[:, s * 128:s * 128 + 64], 1.0), writes=[mskA])
        f.op(pool, lambda: G.memset(mskB[:, s * 128 + 64:s * 128 + 128], 1.0), writes=[mskB])
    h32 = TV(fa[5], fa[5][:, :].rearrange("p (s d) -> p s d", d=128))
    ETb = [ba[3], ba[4]]

    def dump(name, t, ap, shape):
        d = nc.dram_tensor("dump_" + name, list(shape), ap.dtype, kind="ExternalOutput").ap()
        tt = T(None, "dump_" + name)
        f.dma(sp, tt, d, t, ap)
        dump_ts.append(tt)

    def layernorm_rows(s, ln_idx):
        smx = smln[s]
        X = xr[s]
        st = smx[:, 0:12].rearrange("p (a b) -> p a b", b=6)
        for hf in range(2):
            f.op(dve, lambda: V.bn_stats(st[:, hf, :], X[:, hf * 512:(hf + 1) * 512]), reads=[X], writes=[smx])
        f.op(dve, lambda: V.bn_aggr(smx[:, 12:14], smx[:, 0:12]), reads=[smx], writes=[smx])
        f.op(dve, lambda: V.tensor_scalar(smx[:, 14:15], smx[:, 13:14], EPS, None, ALU.add), reads=[smx], writes=[smx])
        f.op(act, lambda: Sx.activation(smx[:, 15:16], smx[:, 14:15], AF.Ln), reads=[smx], writes=[smx])
        f.op(act, lambda: Sx.activation(smx[:, 16:17], smx[:, 15:16], AF.Exp, scale=-0.5), reads=[smx], writes=[smx])

    def layernorm_apply(s):
        smx = smln[s]
        X = xr[s]
        f.op(dve, lambda: V.tensor_scalar(X[:, :], X[:, :], smx[:, 12:13], smx[:, 16:17], ALU.subtract, ALU.mult), reads=[X, smx], writes=[X])
        f.op(dve, lambda: V.tensor_tensor(X[:, :], X[:, :], lnp[:, 0, :], ALU.mult), reads=[X, lnp], writes=[X])
        f.op(pool, lambda: G.tensor_tensor(X[:, :], X[:, :], lnp[:, 1, :], ALU.add), reads=[X, lnp], writes=[X])

    def proj_res_ln(wbuf, srcT, ln_idx):
        f.dma(sp, lnp, lnp[:, :, :], None,
              rows_d[0:1, R_LN + ln_idx * 2048:R_LN + (ln_idx + 1) * 2048].rearrange("o (a d) -> o a d", a=2).partition_broadcast(128))
        for s in range(4):
            for hf in range(2):
                pb = p2 if hf == 0 else p3
                for kc in range(8):
                    f.op(pe, lambda: Tn.matmul(pb[:, :], srcT[:, kc, s * 128:(s + 1) * 128], wbuf[:, kc, hf * 512:(hf + 1) * 512], start=(kc == 0), stop=(kc == 7)),
                         reads=[srcT, wbuf], writes=[pb])
                f.op(dve, lambda: V.scalar_tensor_tensor(xr[s][:, hf * 512:(hf + 1) * 512], xr[s][:, hf * 512:(hf + 1) * 512], ALPHA, pb[:, :], ALU.mult, ALU.add),
                     reads=[xr[s], pb], writes=[xr[s]])
            layernorm_rows(s, ln_idx)
            if s > 0:
                layernorm_apply(s - 1)
        layernorm_apply(3)

    for it in range(ntiles):
        t0 = it * TM
        for s_ in range(4):
            f.dma(sp, xr[s_], xr[s_][:, :], None, x_d[t0 + s_ * 128:t0 + (s_ + 1) * 128, :])
        transposes_to_xT(4, None)
        chk("xT")
        f.dma(sp, Whd[0], Whd[0][:, :, :], whd_sts[0], whd_s[0].rearrange("p (k c) -> p k c", c=512))

        def partA(hd):
            is_hg = hd < 4
            h = hd % 4
            W = Whd[hd % 2]
            brow = browb[hd % 2]
            pvg = pAB[:, :].rearrange("p (s c) -> p s c", c=256)
            qa, ka, qb, kb = ba[0], ba[1], ba[2], ba[3]
            kT_for_tok = kb if is_hg else ka
            q_inter = qb if is_hg else qa
            NV = 128 if is_hg else 129
            W = Whd[hd % 2]
            if hd + 1 < 8:
                f.dma(sp, Whd[(hd + 1) % 2], Whd[(hd + 1) % 2][:, :, :], whd_sts[hd + 1], whd_s[hd + 1].rearrange("p (k c) -> p k c", c=512))
            else:
                f.dma(sp, wsqb[0], wsqb[0][:, :, :], wsq_st, wsq_s[0].rearrange("p (k c) -> p k c", c=1024))
            is_hg = hd < 4
            h = hd % 4
            brow = browb[hd % 2]
            if hd == 4:
                for gi, pb in ((0, p2), (1, p3)):
                    for kc in range(8):
                        f.op(pe, lambda: Tn.matmul(pb[0:4, :], wg[:, kc, gi * 4:gi * 4 + 4], xT[:, kc, :], start=(kc == 0), stop=(kc == 7)),
                             reads=[wg, xT], writes=[pb])
                t1, nb, u, ee, fl = [TV(fa[i], fa[i][0:4, :]) for i in range(1, 6)]
                f.op(act, lambda: Sx.activation(t1[:, :], p3[0:4, :], AF.Exp, bias=pd[0:4, 12:13], scale=-1.0), reads=[p3, pd], writes=[t1])
                f.op(act, lambda: Sx.activation(t1[:, :], t1[:, :], AF.Ln, bias=1.0, scale=1.0), reads=[t1], writes=[t1])
                f.op(dve, lambda: V.tensor_tensor_scan(nb[:, :], ones[0:4, :], t1[:, :], 0.0, ALU.mult, ALU.add), reads=[ones, t1], writes=[nb])
                nbv = nb[:, :].rearrange("p (c t) -> p c t", t=64)
                f.op(dve, lambda: V.memset(g_s[:, 0:16], 0.0), writes=[g_s])
                f.op(dve, lambda: V.tensor_copy(g_s[:, 1:8], nbv[:, 0:7, 63]), reads=[nb], writes=[g_s])
                f.op(dve, lambda: V.tensor_tensor(nbv, nbv, g_s[:, 0:8].unsqueeze(2).to_broadcast([4, 8, 64]), ALU.subtract), reads=[nb, g_s], writes=[nb])
                f.op(dve, lambda: V.tensor_scalar(g_s[:, 9:16], nbv[:, 0:7, 63], -1.0, None, ALU.mult), reads=[nb], writes=[g_s])
                f.op(dve, lambda: V.scalar_tensor_tensor(u[:, :], p2[0:4, :], pp[0:4, 240:241], nb[:, :], ALU.add, ALU.add), reads=[p2, pp, nb], writes=[u])
                uv = u[:, :].rearrange("p (c t) -> p c t", t=64)
                f.op(dve, lambda: V.tensor_reduce(g_s[:, 16:24], uv, AX.X, ALU.max), reads=[u], writes=[g_s])
                f.op(dve, lambda: V.tensor_tensor_scan(g_s[:, 24:32], g_s[:, 8:16], g_s[:, 16:24], mcar[:, 0:1], ALU.add, ALU.max), reads=[g_s, mcar], writes=[g_s])
                f.op(dve, lambda: V.tensor_copy(g_s[:, 32:33], mcar[:, 0:1]), reads=[mcar], writes=[g_s])
                f.op(dve, lambda: V.tensor_tensor(g_s[:, 33:40], g_s[:, 9:16], g_s[:, 24:31], ALU.add), reads=[g_s], writes=[g_s])
                f.op(dve, lambda: V.tensor_tensor(mcar[:, 0:1], g_s[:, 31:32], nbv[:, 7, 63:64], ALU.subtract), reads=[g_s, nb], writes=[mcar])
                f.op(dve, lambda: V.tensor_tensor(g_s[:, 40:48], g_s[:, 32:40], g_s[:, 24:32], ALU.subtract), reads=[g_s], writes=[g_s])
                f.op(dve, lambda: V.tensor_scalar(g_s[:, 40:48], g_s[:, 40:48], -100.0, None, ALU.max), reads=[g_s], writes=[g_s])
                f.op(act, lambda: Sx.activation(g_s[:, 40:48], g_s[:, 40:48], AF.Exp), reads=[g_s], writes=[g_s])
                Rb = g_s[:, 24:32].unsqueeze(2).to_broadcast([4, 8, 64])
                eev = ee[:, :].rearrange("p (c t) -> p c t", t=64)
                flv = fl[:, :].rearrange("p (c t) -> p c t", t=64)
                f.op(dve, lambda: V.tensor_tensor(eev, uv, Rb, ALU.subtract), reads=[u, g_s], writes=[ee])
                f.op(act, lambda: Sx.activation(ee[:, :], ee[:, :], AF.Exp), reads=[ee], writes=[ee])
                f.op(dve, lambda: V.tensor_tensor(flv, nbv, Rb, ALU.subtract), reads=[nb, g_s], writes=[fl])
                f.op(act, lambda: Sx.activation(fl[:, :], fl[:, :], AF.Exp), reads=[fl], writes=[fl])
                for s in range(4):
                    for q, src in ((0, ee), (1, fl)):
                        f.op(pe, lambda: Tn.transpose(p4[:, (s * 2 + q) * 4:(s * 2 + q) * 4 + 4], src[0:4, s * 128:(s + 1) * 128], cst[0:4, 0:4]),
                             reads=[src, cst], writes=[p4])
                f.op(dve, lambda: V.tensor_copy(eftok[:, :, :, :].rearrange("p s q h -> p (s q h)"), p4[:, 0:32]), reads=[p4], writes=[eftok])
                f.op(dve, lambda: V.tensor_tensor(t1[:, 0:32].rearrange("p (h c) -> p h c", c=8), g_s[:, 40:48].unsqueeze(1).to_broadcast([4, 4, 8]),
                                                  selm.rearrange("p (h c) -> p h c", c=8), ALU.mult), reads=[g_s, cst], writes=[t1])
                f.op(dve, lambda: V.tensor_copy(ddb[:, 0, :], t1[:, 0:32]), reads=[t1], writes=[ddb])
                f.op(dve, lambda: V.tensor_copy(t1[:, 32:64], ddb[:, 0, :]), reads=[ddb], writes=[t1])
                f.op(dve, lambda: V.tensor_tensor(ddb[:, 1, :], t1[:, 0:32], t1[:, 32:64], ALU.subtract), reads=[t1], writes=[ddb])
                f.op(pe, lambda: Tn.matmul(p4[:, 64:96], onesb[0:4, 0:128], ddb[:, 0, :], start=True, stop=False), reads=[onesb, ddb], writes=[p4])
                f.op(pe, lambda: Tn.matmul(p4[:, 64:96], onesb[0:4, 0:128], ddb[:, 1, :], start=False, stop=True), reads=[onesb, ddb], writes=[p4])
                f.op(dve, lambda: V.tensor_copy(decbc[:, :], p4[:, 64:96]), reads=[p4], writes=[decbc])

            for blk, pb in ((0, p2), (1, p3)):
                for kc in range(8):
                    f.op(pe, lambda: Tn.matmul(pb[:, :], W[:, kc, blk * 128:(blk + 1) * 128], xT[:, kc, :], start=(kc == 0), stop=(kc == 7)),
                         reads=[W, xT], writes=[pb])
            chk("h%d_fm" % hd)
            pvg = pAB[:, :].rearrange("p (s c) -> p s c", c=256)
            for s in range(4):
                for kc in range(8):
                    f.op(pe, lambda: Tn.matmul(pvg[:, s, :], xT[:, kc, s * 128:(s + 1) * 128], W[:, kc, 256:512], start=(kc == 0), stop=False),
                         reads=[W, xT], writes=[pAB])
                chk("h%d_tm%d" % (hd, s))
                f.op(pe, lambda: Tn.matmul(pvg[:, s, :], onesb[0:1, 0:128], bhl_all[0:1, hd, 0, :], start=False, stop=False),
                     reads=[onesb, bhl_all], writes=[pAB])
                f.op(pe, lambda: Tn.matmul(pvg[:, s, :], onesb[0:1, 0:128], bhl_all[0:1, hd, 1, :], start=False, stop=True),
                     reads=[onesb, bhl_all], writes=[pAB])


        def partB(hd):
            is_hg = hd < 4
            h = hd % 4
            W = Whd[hd % 2]
            brow = browb[hd % 2]
            pvg = pAB[:, :].rearrange("p (s c) -> p s c", c=256)
            qa, ka, qb, kb = ba[0], ba[1], ba[2], ba[3]
            kT_for_tok = kb if is_hg else ka
            q_inter = qb if is_hg else qa
            NV = 128 if is_hg else 129
            chk("h%d_proj" % hd)
            qa, ka, qb, kb = ba[0], ba[1], ba[2], ba[3]
            if is_hg:
                q32, sig, lf, kin, Bc, Dd = fa
                c = h * 4
                f.op(act, lambda: Sx.activation(q32[:, :], p2[:, :], AF.Silu, bias=pp[:, c:c + 1]), reads=[p2, pp], writes=[q32])
                f.op(dve, lambda: V.tensor_copy(vtok[:, :, 0:128], pvg[:, :, 0:128]), reads=[pAB], writes=[vtok])
                f.op(act, lambda: Sx.activation(gw[:, :, :], pvg[:, :, 128:256], AF.Silu), reads=[pAB], writes=[gw])
                f.op(pool, lambda: G.tensor_tensor(gw[:, :, :], gw[:, :, :], hgw[:, h * 128:(h + 1) * 128].unsqueeze(1).to_broadcast([128, 4, 128]), ALU.mult),
                     reads=[gw, hgw], writes=[gw])
                f.op(act, lambda: Sx.activation(sig[:, :], p3[:, :], AF.Sigmoid, bias=pp[:, c + 1:c + 2]), reads=[p3, pp], writes=[sig])
                f.op(act, lambda: Sx.activation(lf[:, :], sig[:, :], AF.Ln, bias=pd[:, h * 3:h * 3 + 1], scale=pd[:, h * 3 + 1:h * 3 + 2]), reads=[sig, pd], writes=[lf])
                f.op(dve, lambda: V.tensor_scalar(kin[:, :], sig[:, :], pd[:, h * 3 + 2:h * 3 + 3], pd[:, h * 3 + 1:h * 3 + 2], ALU.mult, ALU.add), reads=[sig, pd], writes=[kin])
                f.op(dve, lambda: V.tensor_tensor_scan(Bc[:, :], ones[:, :], lf[:, :], 0.0, ALU.mult, ALU.add), reads=[ones, lf], writes=[Bc])
                Bv = Bc[:, :].rearrange("p (c t) -> p c t", t=64)
                f.op(dve, lambda: V.memset(sm2[:, 0:8], 0.0), writes=[sm2])
                f.op(dve, lambda: V.tensor_copy(sm2[:, 1:8], Bv[:, 0:7, 63]), reads=[Bc], writes=[sm2])
                f.op(dve, lambda: V.tensor_tensor(sm2[:, 8:16], Bv[:, :, 31], sm2[:, 0:8], ALU.subtract), reads=[Bc, sm2], writes=[sm2])
                f.op(dve, lambda: V.tensor_tensor(sm2[:, 16:24], Bv[:, :, 63], Bv[:, :, 31], ALU.subtract), reads=[Bc], writes=[sm2])
                f.op(dve, lambda: V.tensor_tensor(sm2[:, 24:32], Bv[:, :, 63], sm2[:, 0:8], ALU.subtract), reads=[Bc, sm2], writes=[sm2])
                f.op(act, lambda: Sx.activation(sm2[:, 8:32], sm2[:, 8:32], AF.Exp), reads=[sm2], writes=[sm2])
                Dv = Dd[:, :].rearrange("p (c t) -> p c t", t=64)
                f.op(dve, lambda: V.tensor_tensor(Dv, Bv, Bv[:, :, 31:32].to_broadcast([128, 8, 64]), ALU.subtract), reads=[Bc], writes=[Dd])
                E1, E1i = lf, sig
                f.op(act, lambda: Sx.activation(E1[:, :], Dd[:, :], AF.Exp), reads=[Dd], writes=[E1])
                f.op(act, lambda: Sx.activation(E1i[:, :], Dd[:, :], AF.Exp, scale=-1.0), reads=[Dd], writes=[E1i])
                f.op(dve, lambda: V.tensor_tensor(E1[:, :], q32[:, :], E1[:, :], ALU.mult), reads=[q32, E1], writes=[E1])
                f.op(dve, lambda: V.tensor_tensor(E1i[:, :], kin[:, :], E1i[:, :], ALU.mult), reads=[kin, E1i], writes=[E1i])
                f.op(pool, lambda: G.tensor_copy(qa[:, :], E1[:, :]), reads=[E1], writes=[qa])
                f.op(pool, lambda: G.tensor_copy(ka[:, :], E1i[:, :]), reads=[E1i], writes=[ka])
                f.op(dve, lambda: V.tensor_tensor(qb[:, :].rearrange("p (c t) -> p c t", t=64), E1[:, :].rearrange("p (c t) -> p c t", t=64),
                                                  sm2[:, 8:16].unsqueeze(2).to_broadcast([128, 8, 64]), ALU.mult), reads=[E1, sm2], writes=[qb])
                f.op(dve, lambda: V.tensor_tensor(kb[:, :].rearrange("p (c t) -> p c t", t=64), E1i[:, :].rearrange("p (c t) -> p c t", t=64),
                                                  sm2[:, 16:24].unsqueeze(2).to_broadcast([128, 8, 64]), ALU.mult), reads=[E1i, sm2], writes=[kb])
                kT_for_tok = kb
                q_inter = qb
                NV = 128
            else:
                cb0 = 16 + h * 12
                for s in range(4):
                    f.op(dve, lambda: V.tensor_scalar(vtok[:, s, 0:128], pvg[:, s, 0:128], eftok[:, s, 0, h:h + 1], None, ALU.mult), reads=[pAB, eftok], writes=[vtok])
                f.op(dve, lambda: V.tensor_copy(vtok[:, :, 128], eftok[:, :, 0, h]), reads=[eftok], writes=[vtok])
                f.op(act, lambda: Sx.activation(gw[:, :, :], pvg[:, :, 128:256], AF.Sigmoid), reads=[pAB], writes=[gw])
                f.op(pool, lambda: G.tensor_tensor(gw[:, :, :], gw[:, :, :], mlw[:, h * 128:(h + 1) * 128].unsqueeze(1).to_broadcast([128, 4, 128]), ALU.mult),
                     reads=[gw, mlw], writes=[gw])
                for blk, pb, dst in ((0, p2, qa), (1, p3, ka)):
                    hb = h * 2 + blk
                    ubm = ubs[blk]
                    ubmh = ubs_h[blk]
                    f.op(dve, lambda: V.tensor_copy(ubmh[:, 0:3], hml[hb][:, :]), reads=[hml[hb]], writes=[ubmh])
                    f.op(act, lambda: Sx.activation(ubm[:, 3:515], pb[:, :], AF.Identity, bias=pp[:, cb0 + blk:cb0 + blk + 1]), reads=[pb, pp], writes=[ubm])
                    f.op(pool, lambda: G.tensor_copy(hml[hb][:, :], ubm[:, 512:515]), reads=[ubm], writes=[hml[hb]])
                    acc = fa[blk]
                    wc = cb0 + 2 + blk * 4
                    f.op(dve, lambda: V.tensor_scalar(acc[:, :], ubm[:, 3:515], pp[:, wc + 3:wc + 4], pp[:, cb0 + 10 + blk:cb0 + 11 + blk], ALU.mult, ALU.add),
                         reads=[ubm, pp], writes=[acc])
                    for j in range(3):
                        f.op(dve, lambda: V.scalar_tensor_tensor(acc[:, :], ubm[:, j:j + 512], pp[:, wc + j:wc + j + 1], acc[:, :], ALU.mult, ALU.add),
                             reads=[ubm, ubmh, pp, acc], writes=[acc])
                    f.op(act, lambda: Sx.activation(acc[:, :], acc[:, :], AF.Silu), reads=[acc], writes=[acc])
                    if blk == 0:
                        f.op(dve, lambda: V.tensor_scalar(dst[:, :], acc[:, :], 128.0 ** -0.5, None, ALU.mult), reads=[acc], writes=[dst])
                    else:
                        f.op(pool, lambda: G.tensor_copy(dst[:, :], acc[:, :]), reads=[acc], writes=[dst])
                kT_for_tok = ka
                q_inter = qa
                NV = 129


        def partC(hd):
            is_hg = hd < 4
            h = hd % 4
            W = Whd[hd % 2]
            brow = browb[hd % 2]
            pvg = pAB[:, :].rearrange("p (s c) -> p s c", c=256)
            qa, ka, qb, kb = ba[0], ba[1], ba[2], ba[3]
            kT_for_tok = kb if is_hg else ka
            q_inter = qb if is_hg else qa
            NV = 128 if is_hg else 129
            chk("h%d_elem" % hd)
            p7v = p7b[:, 0:512].rearrange("p (s d) -> p s d", d=128)
            for s in range(4):
                f.op(pe, lambda: Tn.transpose(p7v[:, s, :], kT_for_tok[:, s * 128:(s + 1) * 128], identb[:, :]), reads=[kT_for_tok, identb], writes=[p7])
            f.op(act, lambda: Sx.copy(ktok[:, :, :], p7v), reads=[p7], writes=[ktok])
            chk("h%d_ktok" % hd)
            p4v = p4[:, :].rearrange("p (s t) -> p s t", t=128)
            for s in range(4):
                f.op(pe, lambda: Tn.matmul(p4v[:, s, :], ka[:, s * 128:(s + 1) * 128], qa[:, s * 128:(s + 1) * 128], start=True, stop=True),
                     reads=[ka, qa], writes=[p4])
            f.op(dve, lambda: V.tensor_tensor(attb[:, :, :], p4v, maskT.unsqueeze(1).to_broadcast([128, 4, 128]), ALU.mult), reads=[p4, cst], writes=[attb])
            chk("h%d_att" % hd)
            pbanks = [[p5, p4], [p6, p7]]
            for c8 in range(8):
                pbk = pbanks[c8 % 2][(c8 // 2) // 3]
                o = ((c8 // 2) % 3) * 132
                lo = (c8 % 2) * 64
                chk("h%d_P%d" % (hd, c8))
                f.op(pe, lambda: Tn.matmul(pbk[:, o:o + NV], ktok[lo:lo + 64, c8 // 2, :], vtok[lo:lo + 64, c8 // 2, 0:NV], start=True, stop=True),
                     reads=[ktok, vtok], writes=[pbk])
            chk("h%d_P" % hd)
            St = Shg[h] if is_hg else Cml[h]
            for c8 in range(8):
                pbk = pbanks[c8 % 2][(c8 // 2) // 3]
                o = ((c8 // 2) % 3) * 132
                if is_hg:
                    f.op(dve, lambda: V.tensor_copy(Sbf[:, c8, 0:NV], St[:, :]), reads=[St], writes=[Sbf])
                    dec = sm2[:, 24 + c8:25 + c8]
                else:
                    dec = decbc[:, h * 8 + c8:h * 8 + c8 + 1]
                    f.op(dve, lambda: V.tensor_scalar(Sbf[:, c8, 0:NV], St[:, :], dec, None, ALU.mult), reads=[St, decbc], writes=[Sbf])
                f.op(dve, lambda: V.scalar_tensor_tensor(St[:, :], St[:, :], dec, pbk[:, o:o + NV], ALU.mult, ALU.add), reads=[St, pbk, sm2, decbc], writes=[St])
            chk("h%d_chain" % hd)
            def pvT(s_):
                return p5 if s_ < 2 else p6

            def pv(s_, a_, b_):
                return pvT(s_)[:, (s_ % 2) * 256 + a_:(s_ % 2) * 256 + b_]
            p5v = p5[:, :].rearrange("p (s c) -> p s c", c=256)
            p6v = p6[:, :].rearrange("p (s c) -> p s c", c=256)
            f.op(pool, lambda: G.tensor_tensor(qpA[:, :], q_inter[:, :], mskA[:, :], ALU.mult), reads=[q_inter, mskA], writes=[qpA])
            f.op(pool, lambda: G.tensor_tensor(qpB[:, :], q_inter[:, :], mskB[:, :], ALU.mult), reads=[q_inter, mskB], writes=[qpB])
            for s in range(4):
                f.op(pe, lambda: Tn.matmul(pv(s, 0, NV), attb[:, s, :], vtok[:, s, 0:NV], start=True, stop=False), reads=[attb, vtok], writes=[pvT(s)])
                f.op(pe, lambda: Tn.matmul(pv(s, 0, NV), qpA[:, s * 128:s * 128 + 128], Sbf[:, 2 * s, 0:NV], start=False, stop=False),
                     reads=[qpA, Sbf], writes=[pvT(s)])
                f.op(pe, lambda: Tn.matmul(pv(s, 0, NV), qpB[:, s * 128:s * 128 + 128], Sbf[:, 2 * s + 1, 0:NV], start=False, stop=True),
                     reads=[qpB, Sbf], writes=[pvT(s)])
            chk("h%d_omm" % hd)
            if is_hg:
                for s in range(4):
                    f.op(act, lambda: Sx.activation(junk[:, :], pv(s, 0, 128), AF.Square, accum_out=sm[:, 20 + s:21 + s]), reads=[pvT(s)], writes=[junk, sm])
                f.op(dve, lambda: V.tensor_scalar(sm[:, 24:28], sm[:, 20:24], 1.0 / 128.0, EPS, ALU.mult, ALU.add), reads=[sm], writes=[sm])
                f.op(act, lambda: Sx.activation(sm[:, 24:28], sm[:, 24:28], AF.Ln), reads=[sm], writes=[sm])
                f.op(act, lambda: Sx.activation(sm[:, 24:28], sm[:, 24:28], AF.Exp, scale=-0.5), reads=[sm], writes=[sm])
                for s in range(4):
                    f.op(dve, lambda: V.scalar_tensor_tensor(ybf[:, s, :], pv(s, 0, 128), sm[:, 24 + s:25 + s], gw[:, s, :], ALU.mult, ALU.mult),
                         reads=[pvT(s), sm, gw], writes=[ybf])
            else:
                f.op(act, lambda: Sx.activation(sm[:, 20:22], p5v[:, :, 128], AF.Abs), reads=[p5], writes=[sm])
                f.op(act, lambda: Sx.activation(sm[:, 22:24], p6v[:, :, 128], AF.Abs), reads=[p6], writes=[sm])
                f.op(dve, lambda: V.tensor_tensor(sm[:, 20:24], sm[:, 20:24], eftok[:, :, 1, h], ALU.max), reads=[sm, eftok], writes=[sm])
                f.op(dve, lambda: V.reciprocal(sm[:, 24:28], sm[:, 20:24]), reads=[sm], writes=[sm])
                f.op(dve, lambda: V.tensor_tensor(h32[:, 0:2, :], p5v[:, :, 0:128], sm[:, 24:26].unsqueeze(2).to_broadcast([128, 2, 128]), ALU.mult),
                     reads=[p5, sm], writes=[h32])
                f.op(dve, lambda: V.tensor_tensor(h32[:, 2:4, :], p6v[:, :, 0:128], sm[:, 26:28].unsqueeze(2).to_broadcast([128, 2, 128]), ALU.mult),
                     reads=[p6, sm], writes=[h32])
                stv = sm[:, 28:52].rearrange("p (s k) -> p s k", k=6)
                for s in range(4):
                    f.op(dve, lambda: V.bn_stats(stv[:, s, :], h32[:, s, :]), reads=[h32], writes=[sm])
                    f.op(dve, lambda: V.bn_aggr(sm[:, 52 + 2 * s:54 + 2 * s], stv[:, s, :]), reads=[sm], writes=[sm])
                mvv = sm[:, 52:60].rearrange("p (s k) -> p s k", k=2)
                f.op(dve, lambda: V.tensor_scalar(sm[:, 60:64], mvv[:, :, 1], EPS, None, ALU.add), reads=[sm], writes=[sm])
                f.op(act, lambda: Sx.activation(sm[:, 60:64], sm[:, 60:64], AF.Ln), reads=[sm], writes=[sm])
                f.op(act, lambda: Sx.activation(sm[:, 60:64], sm[:, 60:64], AF.Exp, scale=-0.5), reads=[sm], writes=[sm])
                f.op(dve, lambda: V.tensor_tensor(gw[:, :, :], gw[:, :, :], sm[:, 60:64].unsqueeze(2).to_broadcast([128, 4, 128]), ALU.mult), reads=[gw, sm], writes=[gw])
                for s in range(4):
                    f.op(dve, lambda: V.scalar_tensor_tensor(ybf[:, s, :], h32[:, s, :], mvv[:, s, 0:1], gw[:, s, :], ALU.subtract, ALU.mult),
                         reads=[h32, sm, gw], writes=[ybf])
            chk("head%d_pre" % hd)
            p7w = p7b[:, 512:1024].rearrange("p (s d) -> p s d", d=128)
            for s in range(4):
                f.op(pe, lambda: Tn.transpose(p7w[:, s, :], ybf[:, s, :], identb[:, :]), reads=[ybf, identb], writes=[p7])
            f.op(act, lambda: Sx.copy(yT[:, hd, :], p7b[:, 512:1024]), reads=[p7], writes=[yT])
            chk("head%d_post" % hd)

        partA(0)
        partB(0)
        for hd in range(8):
            if hd + 1 < 8:
                partA(hd + 1)
            partC(hd)
            if hd + 1 < 8:
                partB(hd + 1)

        chk("mixer")
        if "yT" in dumps and it == ntiles - 1:
            dump("yT", yT, yT[:, :, :], [128, 8, 512])
        f.dma(sp, wsqb[1], wsqb[1][:, :, :], wsq_st, wsq_s[1].rearrange("p (k c) -> p k c", c=1024))
        proj_res_ln(wsqb[0], yT, 0)
        if "x1" in dumps and it == ntiles - 1:
            pass
        transposes_to_xT(4, None)
        chk("ln1")
        for cbk in range(8):
            pb = p2 if cbk % 2 == 0 else p3
            for kc in range(8):
                f.op(pe, lambda: Tn.matmul(pb[:, :], wsqb[1][:, kc, cbk * 128:(cbk + 1) * 128], xT[:, kc, :], start=(kc == 0), stop=(kc == 7)),
                     reads=[wsqb[1], xT], writes=[pb])
            if cbk % 2 == 0:
                f.op(dve, lambda: V.tensor_copy(qTc[:, cbk, :], pb[:, :]), reads=[pb], writes=[qTc])
            else:
                f.op(act, lambda: Sx.copy(qTc[:, cbk, :], pb[:, :]), reads=[pb], writes=[qTc])
        f.dma(sp, wsqb[0], wsqb[0][:, :, :], wsq_st, wsq_s[2].rearrange("p (k c) -> p k c", c=1024))
        for h in range(4):
            for mc in range(2):
                pb = p2 if mc == 0 else p3
                for j in range(2):
                    f.op(pe, lambda: Tn.matmul(pb[:, :], KT[:, 2 * h + j, mc * 128:(mc + 1) * 128], qTc[:, 2 * h + j, :], start=(j == 0), stop=(j == 1)),
                         reads=[KT, qTc], writes=[pb])
                f.op(act, lambda: Sx.activation(ETb[mc][:, :], pb[:, :], AF.Exp, scale=1.0 / 16.0), reads=[pb], writes=[ETb[mc]])
            for mc in range(2):
                f.op(pe, lambda: Tn.matmul(p4[:, :], onesb[:, :], ETb[mc][:, :], start=(mc == 0), stop=(mc == 1)), reads=[onesb, ETb[mc]], writes=[p4])
            rden = fa[0]
            f.op(dve, lambda: V.reciprocal(rden[:, :], p4[:, :]), reads=[p4], writes=[rden])
            for j in range(2):
                pb = p5 if j == 0 else p6
                for mc in range(2):
                    f.op(pe, lambda: Tn.matmul(pb[:, :], Vm[:, mc, (2 * h + j) * 128:(2 * h + j + 1) * 128], ETb[mc][:, :], start=(mc == 0), stop=(mc == 1)),
                         reads=[Vm, ETb[mc]], writes=[pb])
                f.op(dve, lambda: V.tensor_tensor(yT[:, 2 * h + j, :], pb[:, :], rden[:, :], ALU.mult), reads=[pb, rden], writes=[yT])
        proj_res_ln(wsqb[0], yT, 1)
        if "x2" in dumps and it == ntiles - 1:
            pass
        transposes_to_xT(4, None)
        chk("ca")
        pdn = [pAB, pAB, p5, p6]
        def pdn_ap(s):
            return pAB[:, s * 512:(s + 1) * 512] if s < 2 else pdn[s][:, :]
        def ffn_load(g):
            f.dma(sp, wupb[g % 2], wupb[g % 2][:, :, :, :], wup_st, wup_s[g].rearrange("p (j k c) -> p j k c", j=2, k=8))
            f.dma(sp, wdnb[g % 3], wdnb[g % 3][:, :, :], wdn_st, wdn_s[0, g].rearrange("p (j c) -> p j c", j=2))

        def ffn_up(j):
            g, jj = j // 2, j % 2
            if jj == 0 and g + 1 < 11:
                ffn_load(g + 1)
            wu = wupb[g % 2]
            for gv in range(2):
                pb = pup[(2 * j + gv) % 4]
                for kc in range(8):
                    f.op(pe, lambda: Tn.matmul(pb[:, :], wu[:, jj, kc, gv * 128:(gv + 1) * 128], xT[:, kc, :], start=(kc == 0), stop=(kc == 7)),
                         reads=[wu, xT], writes=[pb])

        def ffn_ew(j):
            accs = []
            for gv in range(2):
                pb = pup[(2 * j + gv) % 4]
                ub = ubs[(j % 2) * 2 + gv]
                bidx = j + 22 * gv
                ubh = ubs_h[(j % 2) * 2 + gv]
                f.op(dve, lambda: V.tensor_copy(ubh[:, 0:2], hff[bidx][:, :]), reads=[hff[bidx]], writes=[ubh])
                f.op(act, lambda: Sx.copy(ub[:, 2:514], pb[:, :]), reads=[pb], writes=[ub])
                f.op(pool, lambda: G.tensor_copy(hff[bidx][:, :], ub[:, 512:514]), reads=[ub], writes=[hff[bidx]])
                acc = fa[(j % 2) * 2 + gv]
                pc = 64 + bidx * 4
                f.op(act, lambda: Sx.activation(acc[:, :], pb[:, :], AF.Identity, bias=pp[:, pc + 3:pc + 4], scale=pp[:, pc + 2:pc + 3]), reads=[pb, pp], writes=[acc])
                for t in range(2):
                    f.op(dve, lambda: V.scalar_tensor_tensor(acc[:, :], ub[:, t:t + 512], pp[:, pc + t:pc + t + 1], acc[:, :], ALU.mult, ALU.add),
                         reads=[ub, ubh, pp, acc], writes=[acc])
                accs.append(acc)
            f.op(act, lambda: Sx.activation(accs[0][:, :], accs[0][:, :], AF.Gelu_apprx_tanh), reads=[accs[0]], writes=[accs[0]])
            f.op(dve, lambda: V.tensor_tensor(hTj[j][:, :], accs[0][:, :], accs[1][:, :], ALU.mult), reads=accs, writes=[hTj[j]])

        def ffn_down(j):
            g, jj = j // 2, j % 2
            wd = wdnb[g % 3]
            for s in range(4):
                f.op(pe, lambda: Tn.matmul(pdn_ap(s), hTj[j][:, s * 128:(s + 1) * 128], wd[:, jj, :], start=(j == 0), stop=(j == 21)),
                     reads=[hTj[j], wd], writes=[pdn[s]])

        ffn_load(0)
        ffn_up(0)
        ffn_up(1)
        for j in range(22):
            ffn_ew(j)
            if j + 2 < 22:
                ffn_up(j + 2)
            ffn_down(j)
        f.dma(sp, lnp, lnp[:, :, :], None,
              rows_d[0:1, R_LN + 2 * 2048:R_LN + 3 * 2048].rearrange("o (a d) -> o a d", a=2).partition_broadcast(128))
        for s in range(4):
            f.op(dve, lambda: V.scalar_tensor_tensor(xr[s][:, 0:512], xr[s][:, 0:512], ALPHA, pdn_ap(s), ALU.mult, ALU.add), reads=[xr[s], pdn[s]], writes=[xr[s]])
        for g0 in range(2):
            f.dma(sp, wdnb[g0 % 3], wdnb[g0 % 3][:, :, :], wdn_st, wdn_s[1, g0].rearrange("p (j c) -> p j c", j=2))
        for g in range(11):
            if g + 2 < 11:
                f.dma(sp, wdnb[(g + 2) % 3], wdnb[(g + 2) % 3][:, :, :], wdn_st, wdn_s[1, g + 2].rearrange("p (j c) -> p j c", j=2))
            wd = wdnb[g % 3]
            for jj in range(2):
                j = 2 * g + jj
                for s in range(4):
                    f.op(pe, lambda: Tn.matmul(pdn_ap(s), hTj[j][:, s * 128:(s + 1) * 128], wd[:, jj, :], start=(j == 0), stop=(j == 21)),
                         reads=[hTj[j], wd], writes=[pdn[s]])
        for s in range(4):
            f.op(dve, lambda: V.scalar_tensor_tensor(xr[s][:, 512:1024], xr[s][:, 512:1024], ALPHA, pdn_ap(s), ALU.mult, ALU.add), reads=[xr[s], pdn[s]], writes=[xr[s]])
            layernorm_rows(s, 2)
            if s > 0:
                layernorm_apply(s - 1)
                f.dma(sp, out_t, out_d[t0 + (s - 1) * 128:t0 + s * 128, :], xr[s - 1], xr[s - 1][:, :])
        layernorm_apply(3)
        f.dma(sp, out_t, out_d[t0 + 3 * 128:t0 + 4 * 128, :], xr[3], xr[3][:, :])


def host_prep(inp):
    w_in = inp["w_in"][0]
    b_in = inp["b_in"][0]

    def tile_k(w):
        return w.reshape(8, 128, -1).transpose(1, 0, 2)
    whd = np.empty((8, 128, 8, 512), np.float32)
    brow = np.empty((8, 256), np.float32)
    for h in range(4):
        cols = [slice(0 + h * 128, 128 + h * 128), slice(512 + h * 128, 640 + h * 128), slice(1024 + h * 128, 1152 + h * 128), slice(1536 + h * 128, 1664 + h * 128)]
        whd[h] = np.concatenate([tile_k(w_in[:, c]) for c in cols], axis=2)
        brow[h] = np.concatenate([b_in[cols[2]], b_in[cols[3]]])
        cols = [slice(2048 + h * 128, 2176 + h * 128), slice(2560 + h * 128, 2688 + h * 128), slice(3072 + h * 128, 3200 + h * 128), slice(3584 + h * 128, 3712 + h * 128)]
        whd[4 + h] = np.concatenate([tile_k(w_in[:, c]) for c in cols], axis=2)
        brow[4 + h] = np.concatenate([b_in[cols[2]], b_in[cols[3]]])
    wg = tile_k(w_in[:, 4096:4104]).reshape(128, 64)
    wsq = np.stack([tile_k(inp["w_out"][0]), tile_k(inp["ca_wq"][0]), tile_k(inp["ca_wo"][0])]).reshape(3, 128, 8192)
    wkv = inp["ca_wkv"][0]
    wkv_t = np.stack([tile_k(wkv[:, :1024]), tile_k(wkv[:, 1024:])]).reshape(2, 128, 8192)
    wu = inp["ffn_w_up"][0]
    wup = np.empty((11, 128, 2, 8, 256), np.float32)
    for j in range(22):
        blk = np.concatenate([tile_k(wu[:, j * 128:(j + 1) * 128]), tile_k(wu[:, 2816 + j * 128:2816 + (j + 1) * 128])], axis=2)
        wup[j // 2, :, j % 2] = blk
    wd = inp["ffn_w_down"][0]
    wdn = np.empty((2, 11, 128, 2, 512), np.float32)
    for j in range(22):
        for hf in range(2):
            wdn[hf, j // 2, :, j % 2] = wd[j * 128:(j + 1) * 128, hf * 512:(hf + 1) * 512]
    pp = np.zeros((128, NPP), np.float32)
    lbl = inp["hg_lb_logits"]
    cw = inp["ml_conv_w"][0]; cbias = inp["ml_conv_b"][0]
    for h in range(4):
        sl = slice(h * 128, (h + 1) * 128)
        pp[:, h * 4 + 0] = b_in[0 + h * 128:128 + h * 128]
        pp[:, h * 4 + 1] = b_in[512 + h * 128:640 + h * 128]
        pp[:, h * 4 + 2] = lbl[0, sl]
        pp[:, h * 4 + 3] = lbl[1, sl]
        c0 = 16 + h * 12
        pp[:, c0 + 0] = b_in[2048 + h * 128:2176 + h * 128]
        pp[:, c0 + 1] = b_in[2560 + h * 128:2688 + h * 128]
        for blk in range(2):
            csl = slice(blk * 512 + h * 128, blk * 512 + (h + 1) * 128)
            for j in range(4):
                pp[:, c0 + 2 + blk * 4 + j] = cw[j, csl]
            pp[:, c0 + 10 + blk] = cbias[csl]
    fw = inp["ffn_conv_w"][0]; fb = inp["ffn_conv_b"][0]
    for b in range(44):
        sl = slice(b * 128, (b + 1) * 128)
        for j in range(3):
            pp[:, 64 + b * 4 + j] = fw[j, sl]
        pp[:, 64 + b * 4 + 3] = fb[sl]
    pp[0:4, 240] = b_in[4096:4100]
    pp[0:4, 241] = b_in[4100:4104]
    rows = np.concatenate([inp["hg_norm_w"][0], inp["ml_norm_w"][0], inp["ln1_g"][0], inp["ln1_b"][0], inp["ln2_g"][0], inp["ln2_b"][0],
                           inp["ln3_g"][0], inp["ln3_b"][0], brow.reshape(-1)]).astype(np.float32)[None, :]
    cst = np.zeros((128, 512), np.float32)
    cst[:, 0:128] = np.eye(128, dtype=np.float32)
    idx = np.arange(128)
    cst[:, 128:256] = ((idx[:, None] // 64 == idx[None, :] // 64) & (idx[:, None] <= idx[None, :])).astype(np.float32)
    for k in range(4):
        cst[k, 256 + k * 8:256 + (k + 1) * 8] = 1.0
    shared = dict(whd=np.ascontiguousarray(whd.reshape(8, 128, 4096)), wg=np.ascontiguousarray(wg), wsq=np.ascontiguousarray(wsq),
                  wkv=np.ascontiguousarray(wkv_t), wup=np.ascontiguousarray(wup.reshape(11, 128, 4096)),
                  wdn=np.ascontiguousarray(wdn.reshape(2, 11, 128, 1024)), pp=pp, rows=np.ascontiguousarray(rows), cst=cst)
    return shared


_NC_CACHE = {}


def kernel(**inputs):
    inp = {k: np.asarray(v) for k, v in inputs.items()}
    shared = host_prep(inp)
    if "nc" not in _NC_CACHE:
        _NC_CACHE["nc"] = build()
    nc = _NC_CACHE["nc"]
    in_maps = []
    for b in range(8):
        m = dict(shared)
        m["x"] = np.ascontiguousarray(inp["x"][b])
        m["mem"] = np.ascontiguousarray(inp["mem"][b])
        in_maps.append(m)
    res = run_bass_kernel_spmd(nc, in_maps, core_ids=list(range(8)))
    return np.stack([np.asarray(r["out"]) for r in res.results]).astype(np.float32)
```

```python
import numpy as np
import concourse.bass as bass
import concourse.mybir as mybir
from concourse.bass_utils import run_bass_kernel_spmd

F32 = mybir.dt.float32
BF16 = mybir.dt.bfloat16
AF = mybir.ActivationFunctionType
ALU = mybir.AluOpType
AX = mybir.AxisListType

S = 4096
D = 1024
TM = 512
NTILES = S // TM
ALPHA = 2.0 ** 0.25
EPS = 1e-5
NPP = 256
R_HGW, R_MLW, R_LN, R_B = 0, 512, 1024, 1024 + 6 * 1024
NR = R_B + 2048


class Eng:
    def __init__(self, name, eng, sem, inc=1):
        self.name, self.eng, self.sem, self.inc = name, eng, sem, inc
        self.count = 0
        self.waited = {}


class T:
    def __init__(self, h, name=None):
        self.h = h
        self.name = name
        self.w = None
        self.r = {}
        self.dma = None
        self.psum = False
        self.group = [self]

    def __getitem__(self, k):
        return self.h[k]


class TV:
    def __init__(self, parent, ap):
        self.__dict__["p"] = parent
        self.__dict__["h"] = ap

    def __getitem__(self, k):
        return self.h[k]

    def __getattr__(self, k):
        return getattr(self.__dict__["p"], k)

    def __setattr__(self, k, v):
        setattr(self.__dict__["p"], k, v)


def alias(*ts):
    g = []
    for t in ts:
        for u in t.group:
            if u not in g:
                g.append(u)
    for t in g:
        t.group = g


class FW:
    def __init__(self, nc):
        self.nc = nc
        self.pe = self._mk("pe", nc.tensor)
        self.act = self._mk("act", nc.scalar)
        self.dve = self._mk("dve", nc.vector)
        self.pool = self._mk("pool", nc.gpsimd)
        self.sp = self._mk("sp", nc.sync)
        self.ndma = 0

    def _mk(self, name, eng, inc=1):
        sem = self.nc.semaphore(name).__enter__()
        return Eng(name, eng, sem, inc)

    def sb(self, name, shape, dt):
        return T(self.nc.alloc_sbuf_tensor("sb_" + name, list(shape), dt), name)

    def ps(self, name, shape, dt=F32):
        t = T(self.nc.alloc_psum_tensor("ps_" + name, list(shape), dt), name)
        t.psum = True
        return t

    def _deps(self, reads, writes, E=None):
        deps = {}

        def add(e, c):
            if deps.get(e, 0) < c:
                deps[e] = c
        for t0 in reads:
            for t in t0.group:
                if t.w:
                    add(*t.w)
                if t.psum:
                    for e, c in t.r.items():
                        if e is not E:
                            add(e, c)
        for t0 in writes:
            for t in t0.group:
                if t.w:
                    add(*t.w)
                for e, c in t.r.items():
                    add(e, c)
        return deps

    def _wait(self, E, deps, skip_self=False):
        for e, c in deps.items():
            if e is E and skip_self:
                continue
            if E.waited.get(e, 0) < c:
                E.eng.wait_ge(e.sem, c * e.inc)
                E.waited[e] = c

    def op(self, E, fn, reads=(), writes=()):
        deps = self._deps(reads, writes, E)
        self._wait(E, deps, skip_self=(E is self.pe))
        inst = fn()
        E.count += 1
        inst.then_inc(E.sem, 1)
        for t in reads:
            if t.r.get(E, 0) < E.count:
                t.r[E] = E.count
        for t in writes:
            t.w = (E, E.count)
            t.r = {}
        return inst

    def dma(self, E, out_t, out_ap, in_t, in_ap, **kw):
        tgt = out_t if out_t is not None else in_t
        if tgt.dma is None:
            tgt.dma = self._mk("dma%d" % self.ndma, None, inc=16)
            self.ndma += 1
        Dq = tgt.dma
        reads = [in_t] if in_t is not None else []
        writes = [out_t] if out_t is not None else []
        deps = self._deps(reads, writes)
        self._wait(E, deps)
        inst = E.eng.dma_start(out=out_ap, in_=in_ap, **kw)
        Dq.count += 1
        inst.then_inc(Dq.sem, 16)
        for t in reads:
            if t.r.get(Dq, 0) < Dq.count:
                t.r[Dq] = Dq.count
        for t in writes:
            t.w = (Dq, Dq.count)
            t.r = {}
        return inst

    def finish(self, E, ts):
        deps = {}
        for t in ts:
            if t.w and deps.get(t.w[0], 0) < t.w[1]:
                deps[t.w[0]] = t.w[1]
        for e, c in deps.items():
            E.eng.wait_ge(e.sem, c * e.inc)


class _Stop(Exception):
    pass


def build(ntiles=NTILES, dumps=(), stop=None):
    nc = bass.Bass("TRN2", target_bir_lowering=False)
    f = FW(nc)
    try:
        _build_body(nc, f, ntiles, dumps, stop)
    except _Stop:
        pass
    f.finish(f.sp, f.final_ts)
    return nc


def _build_body(nc, f, ntiles, dumps, stop):
    def chk(tag):
        if stop == tag:
            raise _Stop()
    pe, act, dve, pool, sp = f.pe, f.act, f.dve, f.pool, f.sp
    V, Sx, Tn, G = nc.vector, nc.scalar, nc.tensor, nc.gpsimd

    def din(name, shape, dt=F32):
        return nc.dram_tensor(name, list(shape), dt, kind="ExternalInput").ap()

    x_d = din("x", [S, D])
    mem_d = din("mem", [256, D])
    whd_d = din("whd", [8, 128, 4096])
    wg_d = din("wg", [128, 64])
    wsq_d = din("wsq", [3, 128, 8192])
    wkv_d = din("wkv", [2, 128, 8192])
    wup_d = din("wup", [11, 128, 4096])
    wdn_d = din("wdn", [2, 11, 128, 1024])
    pp_d = din("pp", [128, NPP])
    rows_d = din("rows", [1, NR])
    cst_d = din("cst", [128, 512])
    out_d = nc.dram_tensor("out", [S, D], F32, kind="ExternalOutput").ap()
    out_t = T(None, "out")
    dump_ts = [out_t]
    f.final_ts = dump_ts

    def scratch(name, shape):
        return nc.dram_tensor(name, list(shape), BF16, kind="Internal").ap()
    whd_s = scratch("whd_s", [8, 128, 4096]); whd_st = T(None, "whd_s")
    wsq_s = scratch("wsq_s", [3, 128, 8192]); wsq_st = T(None, "wsq_s")
    wkv_s = scratch("wkv_s", [2, 128, 8192]); wkv_st = T(None, "wkv_s")
    wup_s = scratch("wup_s", [11, 128, 4096]); wup_st = T(None, "wup_s")
    wdn_s = scratch("wdn_s", [2, 11, 128, 1024]); wdn_st = T(None, "wdn_s")

    whd_sts = [T(None, "whd_s%d" % h) for h in range(8)]
    for i in range(2):
        f.dma(pool, wkv_st, wkv_s[i], None, wkv_d[i])
    for h in range(8):
        f.dma(pool, whd_sts[h], whd_s[h], None, whd_d[h])
    for i in range(3):
        f.dma(pool, wsq_st, wsq_s[i], None, wsq_d[i])
    for g in range(11):
        f.dma(pool, wup_st, wup_s[g], None, wup_d[g])
        for hf in range(2):
            f.dma(pool, wdn_st, wdn_s[hf, g], None, wdn_d[hf, g])

    chk("prologue")
    cst = f.sb("cst", [128, 512], F32)
    f.dma(sp, cst, cst[:, :], None, cst_d)
    ident = cst[:, 0:128]
    maskT = cst[:, 128:256]
    selm = cst[0:4, 256:288]
    identb = f.sb("identb", [128, 128], BF16)
    f.dma(pool, identb, identb[:, :], None, cst_d[:, 0:128])
    pp = f.sb("pp", [128, NPP], F32)
    f.dma(sp, pp, pp[:, :], None, pp_d)
    wg = f.sb("wg", [128, 8, 8], BF16)
    f.dma(pool, wg, wg[:, :, :], None, wg_d.rearrange("p (k c) -> p k c", c=8))
    ones = f.sb("ones", [128, 512], F32)
    f.op(pool, lambda: G.memset(ones[:, :], 1.0), writes=[ones])
    onesb = f.sb("onesb", [128, 128], BF16)
    f.op(pool, lambda: G.memset(onesb[:, :], 1.0), writes=[onesb])
    hgw = f.sb("hgw", [128, 512], F32)
    f.dma(sp, hgw, hgw[:, :], None, rows_d[0:1, R_HGW:R_HGW + 512].partition_broadcast(128))
    mlw = f.sb("mlw", [128, 512], F32)
    f.dma(sp, mlw, mlw[:, :], None, rows_d[0:1, R_MLW:R_MLW + 512].partition_broadcast(128))
    browb = [f.sb("brow%d" % i, [1, 256], F32) for i in range(2)]
    btmp = f.sb("btmp", [1, 256], F32)
    bhl_all = f.sb("bhl_all", [1, 8, 2, 256], BF16)
    for hd_ in range(8):
        brow_ = browb[hd_ % 2]
        f.dma(sp, brow_, brow_[:, :], None, rows_d[0:1, R_B + hd_ * 256:R_B + (hd_ + 1) * 256])
        f.op(dve, lambda: nc.vector.tensor_copy(bhl_all[:, hd_, 0, :], brow_[:, :]), reads=[brow_], writes=[bhl_all])
        f.op(dve, lambda: nc.vector.tensor_copy(btmp[:, :], bhl_all[:, hd_, 0, :]), reads=[bhl_all], writes=[btmp])
        f.op(dve, lambda: nc.vector.tensor_tensor(bhl_all[:, hd_, 1, :], brow_[:, :], btmp[:, :], ALU.subtract), reads=[brow_, btmp], writes=[bhl_all])
    lnp = f.sb("lnp", [128, 2, 1024], F32)

    pd = f.sb("pd", [128, 16], F32)
    for h in range(4):
        c = h * 4
        f.op(dve, lambda: V.tensor_tensor(pd[:, 13:14], pp[:, c + 2:c + 3], pp[:, c + 3:c + 4], ALU.subtract), reads=[pp], writes=[pd])
        f.op(act, lambda: Sx.activation(pd[:, h * 3:h * 3 + 1], pd[:, 13:14], AF.Sigmoid), reads=[pd], writes=[pd])
        f.op(dve, lambda: V.tensor_scalar(pd[:, h * 3 + 1:h * 3 + 2], pd[:, h * 3:h * 3 + 1], -1.0, 1.0, ALU.mult, ALU.add), reads=[pd], writes=[pd])
        f.op(dve, lambda: V.tensor_scalar(pd[:, h * 3 + 2:h * 3 + 3], pd[:, h * 3:h * 3 + 1], 1.0, -1.0, ALU.mult, ALU.add), reads=[pd], writes=[pd])
    f.op(dve, lambda: V.tensor_scalar(pd[:, 12:13], pp[:, 241:242], -1.0, None, ALU.mult), reads=[pp], writes=[pd])

    chk("consts")
    pAB = f.ps("pAB", [128, 1024])
    p2 = f.ps("p2", [128, 512]); p3 = f.ps("p3", [128, 512]); p4 = f.ps("p4", [128, 512])
    p5 = f.ps("p5", [128, 512]); p6 = f.ps("p6", [128, 512])
    p7 = f.ps("p7", [128, 512])
    p7b = TV(p7, p7.h.bitcast(BF16))

    Shg = [f.sb("Shg%d" % h, [128, 128], F32) for h in range(4)]
    Cml = [f.sb("Cml%d" % h, [128, 129], F32) for h in range(4)]
    for h in range(4):
        f.op(pool, lambda: G.memset(Shg[h][:, :], 0.0), writes=[Shg[h]])
        f.op(pool, lambda: G.memset(Cml[h][:, :], 0.0), writes=[Cml[h]])
    mcar = f.sb("mcar", [4, 1], F32)
    f.op(pool, lambda: G.memset(mcar[:, :], -1e30), writes=[mcar])
    halo_ml = f.sb("halo_ml", [128, 8, 3], F32)
    f.op(pool, lambda: G.memset(halo_ml[:, :, :], 0.0), writes=[halo_ml])
    halo_ff = f.sb("halo_ff", [128, 44, 2], F32)
    f.op(pool, lambda: G.memset(halo_ff[:, :, :], 0.0), writes=[halo_ff])

    KT = f.sb("KT", [128, 8, 256], BF16)
    Vm = f.sb("Vm", [128, 2, 1024], BF16)
    xres = f.sb("xres", [128, 4, 1024], F32)
    xr = [T(xres.h[:, s_, :], "xr%d" % s_) for s_ in range(4)]
    smln = [f.sb("smln%d" % s_, [128, 32], F32) for s_ in range(4)]
    xT = f.sb("xT", [128, 8, 512], BF16)
    wsqb = [f.sb("wsq%d" % i, [128, 8, 1024], BF16) for i in range(2)]

    def transposes_to_xT(nsub, src):
        for s in range(nsub):
            if s % 2 == 0:
                for kc in range(8):
                    f.op(pe, lambda: Tn.transpose(pAB[:, kc * 128:(kc + 1) * 128], xr[s][:, kc * 128:(kc + 1) * 128], ident),
                         reads=[xr[s], cst], writes=[pAB])
                f.op(dve, lambda: V.tensor_copy(xT[:, :, s * 128:(s + 1) * 128], pAB[:, :].rearrange("p (k t) -> p k t", t=128)),
                     reads=[pAB], writes=[xT])
            else:
                for kc in range(8):
                    pb = p5 if kc < 4 else p6
                    f.op(pe, lambda: Tn.transpose(pb[:, (kc % 4) * 128:(kc % 4 + 1) * 128], xr[s][:, kc * 128:(kc + 1) * 128], ident),
                         reads=[xr[s], cst], writes=[pb])
                f.op(act, lambda: Sx.copy(xT[:, 0:4, s * 128:(s + 1) * 128], p5[:, :].rearrange("p (k t) -> p k t", t=128)),
                     reads=[p5], writes=[xT])
                f.op(act, lambda: Sx.copy(xT[:, 4:8, s * 128:(s + 1) * 128], p6[:, :].rearrange("p (k t) -> p k t", t=128)),
                     reads=[p6], writes=[xT])

    for s_ in range(2):
        f.dma(sp, xr[s_], xr[s_][:, :], None, mem_d[s_ * 128:(s_ + 1) * 128, :])
    transposes_to_xT(2, None)
    for i in range(2):
        f.dma(sp, wsqb[i], wsqb[i][:, :, :], wkv_st, wkv_s[i].rearrange("p (k c) -> p k c", c=1024))
    for cb in range(8):
        for kc in range(8):
            f.op(pe, lambda: Tn.matmul(p2[:, 0:256], wsqb[0][:, kc, cb * 128:(cb + 1) * 128], xT[:, kc, 0:256], start=(kc == 0), stop=(kc == 7)),
                 reads=[wsqb[0], xT], writes=[p2])
        f.op(dve, lambda: V.tensor_copy(KT[:, cb, :], p2[:, 0:256]), reads=[p2], writes=[KT])
    for mc in range(2):
        for hf in range(2):
            for kc in range(8):
                f.op(pe, lambda: Tn.matmul(p3[:, :], xT[:, kc, mc * 128:(mc + 1) * 128], wsqb[1][:, kc, hf * 512:(hf + 1) * 512], start=(kc == 0), stop=(kc == 7)),
                     reads=[wsqb[1], xT], writes=[p3])
            f.op(act, lambda: Sx.copy(Vm[:, mc, hf * 512:(hf + 1) * 512], p3[:, :]), reads=[p3], writes=[Vm])

    chk("kv")
    Whd = [f.sb("Whd%d" % i, [128, 8, 512], BF16) for i in range(2)]
    yT = f.sb("yT", [128, 8, 512], BF16)
    fa = [f.sb("fa%d" % i, [128, 512], F32) for i in range(6)]
    ubuf = f.sb("ubuf", [128, 520], F32)
    ba = [f.sb("ba%d" % i, [128, 512], BF16) for i in range(5)]
    attb = f.sb("attb", [128, 4, 128], BF16)
    ktok = f.sb("ktok", [128, 4, 128], BF16)
    vtokb = [f.sb("vtok%d" % i, [128, 4, 132], BF16) for i in range(2)]
    gwb = [f.sb("gw%d" % i, [128, 4, 128], F32) for i in range(2)]
    ybf = f.sb("ybf", [128, 4, 128], BF16)
    Sbf = f.sb("Sbf", [128, 8, 132], BF16)
    sm = f.sb("sm", [128, 64], F32)
    sm2 = f.sb("sm2", [128, 64], F32)
    junk = f.sb("junk", [128, 128], F32)
    g_s = f.sb("g_s", [4, 64], F32)
    eftok = f.sb("eftok", [128, 4, 2, 4], F32)
    decbc = f.sb("decbc", [128, 32], F32)
    ddb = f.sb("ddb", [4, 2, 32], BF16)
    hT = f.sb("hT", [128, 22, 512], BF16)
    wupb = [f.sb("wup%d" % i, [128, 2, 8, 256], BF16) for i in range(2)]
    wdnb = [f.sb("wdn%d" % i, [128, 2, 512], BF16) for i in range(3)]
    ub2 = f.sb("ub2", [128, 520], F32)
    hTj = [T(hT.h[:, j, :], "hT%d" % j) for j in range(22)]
    qTc = T(hT.h[:, 0:8, :], "qTc")
    qTc.group = [qTc] + hTj[0:8]
    for j in range(8):
        hTj[j].group = [hTj[j], qTc]
    ubs = [ubuf, ub2, f.sb("ub3", [128, 520], F32), f.sb("ub4", [128, 520], F32)]
    ubs_h = [T(u.h[:, 0:8], "ubh") for u in ubs]
    hff = [T(halo_ff.h[:, b, :], "hff%d" % b) for b in range(44)]
    hml = [T(halo_ml.h[:, b, :], "hml%d" % b) for b in range(8)]
    for t_ in hff:
        t_.w = halo_ff.w
    for t_ in hml:
        t_.w = halo_ml.w
    pup = [p2, p3, p4, p7]
    qpA = f.sb("qpA", [128, 512], BF16)
    qpB = f.sb("qpB", [128, 512], BF16)
    mskA = f.sb("mskA", [128, 512], BF16)
    mskB = f.sb("mskB", [128, 512], BF16)
    f.op(pool, lambda: G.memset(mskA[:, :], 0.0), writes=[mskA])
    f.op(pool, lambda: G.memset(mskB[:, :], 0.0), writes=[mskB])
    for s in range(4):
        f.op(pool, lambda: G.memset(mskA[:, s * 128:s * 128 + 64], 1.0), writes=[mskA])
        f.op(pool, lambda: G.memset(mskB[:, s * 128 + 64:s * 128 + 128], 1.0), writes=[mskB])
    h32 = TV(fa[5], fa[5][:, :].rearrange("p (s d) -> p s d", d=128))
    ETb = [ba[3], ba[4]]

    def dump(name, t, ap, shape):
        d = nc.dram_tensor("dump_" + name, list(shape), ap.dtype, kind="ExternalOutput").ap()
        tt = T(None, "dump_" + name)
        f.dma(sp, tt, d, t, ap)
        dump_ts.append(tt)

    def layernorm_rows(s, ln_idx):
        smx = smln[s]
        X = xr[s]
        st = smx[:, 0:12].rearrange("p (a b) -> p a b", b=6)
        for hf in range(2):
            f.op(dve, lambda: V.bn_stats(st[:, hf, :], X[:, hf * 512:(hf + 1) * 512]), reads=[X], writes=[smx])
        f.op(dve, lambda: V.bn_aggr(smx[:, 12:14], smx[:, 0:12]), reads=[smx], writes=[smx])
        f.op(dve, lambda: V.tensor_scalar(smx[:, 14:15], smx[:, 13:14], EPS, None, ALU.add), reads=[smx], writes=[smx])
        f.op(act, lambda: Sx.activation(smx[:, 15:16], smx[:, 14:15], AF.Ln), reads=[smx], writes=[smx])
        f.op(act, lambda: Sx.activation(smx[:, 16:17], smx[:, 15:16], AF.Exp, scale=-0.5), reads=[smx], writes=[smx])

    def layernorm_apply(s):
        smx = smln[s]
        X = xr[s]
        f.op(dve, lambda: V.tensor_scalar(X[:, :], X[:, :], smx[:, 12:13], smx[:, 16:17], ALU.subtract, ALU.mult), reads=[X, smx], writes=[X])
        f.op(dve, lambda: V.tensor_tensor(X[:, :], X[:, :], lnp[:, 0, :], ALU.mult), reads=[X, lnp], writes=[X])
        f.op(pool, lambda: G.tensor_tensor(X[:, :], X[:, :], lnp[:, 1, :], ALU.add), reads=[X, lnp], writes=[X])

    def proj_res_ln(wbuf, srcT, ln_idx):
        f.dma(sp, lnp, lnp[:, :, :], None,
              rows_d[0:1, R_LN + ln_idx * 2048:R_LN + (ln_idx + 1) * 2048].rearrange("o (a d) -> o a d", a=2).partition_broadcast(128))
        for s in range(4):
            for hf in range(2):
                pb = p2 if hf == 0 else p3
                for kc in range(8):
                    f.op(pe, lambda: Tn.matmul(pb[:, :], srcT[:, kc, s * 128:(s + 1) * 128], wbuf[:, kc, hf * 512:(hf + 1) * 512], start=(kc == 0), stop=(kc == 7)),
                         reads=[srcT, wbuf], writes=[pb])
                f.op(dve, lambda: V.scalar_tensor_tensor(xr[s][:, hf * 512:(hf + 1) * 512], xr[s][:, hf * 512:(hf + 1) * 512], ALPHA, pb[:, :], ALU.mult, ALU.add),
                     reads=[xr[s], pb], writes=[xr[s]])
            layernorm_rows(s, ln_idx)
            if s > 0:
                layernorm_apply(s - 1)
        layernorm_apply(3)

    for it in range(ntiles):
        t0 = it * TM
        for s_ in range(4):
            f.dma(sp, xr[s_], xr[s_][:, :], None, x_d[t0 + s_ * 128:t0 + (s_ + 1) * 128, :])
        transposes_to_xT(4, None)
        chk("xT")
        f.dma(sp, Whd[0], Whd[0][:, :, :], whd_sts[0], whd_s[0].rearrange("p (k c) -> p k c", c=512))

        def partA(hd):
            is_hg = hd < 4
            h = hd % 4
            W = Whd[hd % 2]
            brow = browb[hd % 2]
            pvg = pAB[:, :].rearrange("p (s c) -> p s c", c=256)
            qa, ka, qb, kb = ba[0], ba[1], ba[2], ba[3]
            kT_for_tok = kb if is_hg else ka
            q_inter = qb if is_hg else qa
            NV = 128 if is_hg else 129
            vtok = vtokb[hd % 2]
            gw = gwb[hd % 2]
            W = Whd[hd % 2]
            if hd + 1 < 8:
                f.dma(sp, Whd[(hd + 1) % 2], Whd[(hd + 1) % 2][:, :, :], whd_sts[hd + 1], whd_s[hd + 1].rearrange("p (k c) -> p k c", c=512))
            else:
                f.dma(sp, wsqb[0], wsqb[0][:, :, :], wsq_st, wsq_s[0].rearrange("p (k c) -> p k c", c=1024))
            is_hg = hd < 4
            h = hd % 4
            brow = browb[hd % 2]
            if hd == 4:
                for gi, pb in ((0, p2), (1, p3)):
                    for kc in range(8):
                        f.op(pe, lambda: Tn.matmul(pb[0:4, :], wg[:, kc, gi * 4:gi * 4 + 4], xT[:, kc, :], start=(kc == 0), stop=(kc == 7)),
                             reads=[wg, xT], writes=[pb])
                t1, nb, u, ee, fl = [TV(fa[i], fa[i][0:4, :]) for i in range(1, 6)]
                f.op(act, lambda: Sx.activation(t1[:, :], p3[0:4, :], AF.Exp, bias=pd[0:4, 12:13], scale=-1.0), reads=[p3, pd], writes=[t1])
                f.op(act, lambda: Sx.activation(t1[:, :], t1[:, :], AF.Ln, bias=1.0, scale=1.0), reads=[t1], writes=[t1])
                f.op(dve, lambda: V.tensor_tensor_scan(nb[:, :], ones[0:4, :], t1[:, :], 0.0, ALU.mult, ALU.add), reads=[ones, t1], writes=[nb])
                nbv = nb[:, :].rearrange("p (c t) -> p c t", t=64)
                f.op(dve, lambda: V.memset(g_s[:, 0:16], 0.0), writes=[g_s])
                f.op(dve, lambda: V.tensor_copy(g_s[:, 1:8], nbv[:, 0:7, 63]), reads=[nb], writes=[g_s])
                f.op(dve, lambda: V.tensor_tensor(nbv, nbv, g_s[:, 0:8].unsqueeze(2).to_broadcast([4, 8, 64]), ALU.subtract), reads=[nb, g_s], writes=[nb])
                f.op(dve, lambda: V.tensor_scalar(g_s[:, 9:16], nbv[:, 0:7, 63], -1.0, None, ALU.mult), reads=[nb], writes=[g_s])
                f.op(dve, lambda: V.scalar_tensor_tensor(u[:, :], p2[0:4, :], pp[0:4, 240:241], nb[:, :], ALU.add, ALU.add), reads=[p2, pp, nb], writes=[u])
                uv = u[:, :].rearrange("p (c t) -> p c t", t=64)
                f.op(dve, lambda: V.tensor_reduce(g_s[:, 16:24], uv, AX.X, ALU.max), reads=[u], writes=[g_s])
                f.op(dve, lambda: V.tensor_tensor_scan(g_s[:, 24:32], g_s[:, 8:16], g_s[:, 16:24], mcar[:, 0:1], ALU.add, ALU.max), reads=[g_s, mcar], writes=[g_s])
                f.op(dve, lambda: V.tensor_copy(g_s[:, 32:33], mcar[:, 0:1]), reads=[mcar], writes=[g_s])
                f.op(dve, lambda: V.tensor_tensor(g_s[:, 33:40], g_s[:, 9:16], g_s[:, 24:31], ALU.add), reads=[g_s], writes=[g_s])
                f.op(dve, lambda: V.tensor_tensor(mcar[:, 0:1], g_s[:, 31:32], nbv[:, 7, 63:64], ALU.subtract), reads=[g_s, nb], writes=[mcar])
                f.op(dve, lambda: V.tensor_tensor(g_s[:, 40:48], g_s[:, 32:40], g_s[:, 24:32], ALU.subtract), reads=[g_s], writes=[g_s])
                f.op(dve, lambda: V.tensor_scalar(g_s[:, 40:48], g_s[:, 40:48], -100.0, None, ALU.max), reads=[g_s], writes=[g_s])
                f.op(act, lambda: Sx.activation(g_s[:, 40:48], g_s[:, 40:48], AF.Exp), reads=[g_s], writes=[g_s])
                Rb = g_s[:, 24:32].unsqueeze(2).to_broadcast([4, 8, 64])
                eev = ee[:, :].rearrange("p (c t) -> p c t", t=64)
                flv = fl[:, :].rearrange("p (c t) -> p c t", t=64)
                f.op(dve, lambda: V.tensor_tensor(eev, uv, Rb, ALU.subtract), reads=[u, g_s], writes=[ee])
                f.op(act, lambda: Sx.activation(ee[:, :], ee[:, :], AF.Exp), reads=[ee], writes=[ee])
                f.op(dve, lambda: V.tensor_tensor(flv, nbv, Rb, ALU.subtract), reads=[nb, g_s], writes=[fl])
                f.op(act, lambda: Sx.activation(fl[:, :], fl[:, :], AF.Exp), reads=[fl], writes=[fl])
                for s in range(4):
                    for q, src in ((0, ee), (1, fl)):
                        f.op(pe, lambda: Tn.transpose(p4[:, (s * 2 + q) * 4:(s * 2 + q) * 4 + 4], src[0:4, s * 128:(s + 1) * 128], cst[0:4, 0:4]),
                             reads=[src, cst], writes=[p4])
                f.op(dve, lambda: V.tensor_copy(eftok[:, :, :, :].rearrange("p s q h -> p (s q h)"), p4[:, 0:32]), reads=[p4], writes=[eftok])
                f.op(dve, lambda: V.tensor_tensor(t1[:, 0:32].rearrange("p (h c) -> p h c", c=8), g_s[:, 40:48].unsqueeze(1).to_broadcast([4, 4, 8]),
                                                  selm.rearrange("p (h c) -> p h c", c=8), ALU.mult), reads=[g_s, cst], writes=[t1])
                f.op(dve, lambda: V.tensor_copy(ddb[:, 0, :], t1[:, 0:32]), reads=[t1], writes=[ddb])
                f.op(dve, lambda: V.tensor_copy(t1[:, 32:64], ddb[:, 0, :]), reads=[ddb], writes=[t1])
                f.op(dve, lambda: V.tensor_tensor(ddb[:, 1, :], t1[:, 0:32], t1[:, 32:64], ALU.subtract), reads=[t1], writes=[ddb])
                f.op(pe, lambda: Tn.matmul(p4[:, 64:96], onesb[0:4, 0:128], ddb[:, 0, :], start=True, stop=False), reads=[onesb, ddb], writes=[p4])
                f.op(pe, lambda: Tn.matmul(p4[:, 64:96], onesb[0:4, 0:128], ddb[:, 1, :], start=False, stop=True), reads=[onesb, ddb], writes=[p4])
                f.op(dve, lambda: V.tensor_copy(decbc[:, :], p4[:, 64:96]), reads=[p4], writes=[decbc])

            for blk, pb in ((0, p2), (1, p3)):
                for kc in range(8):
                    f.op(pe, lambda: Tn.matmul(pb[:, :], W[:, kc, blk * 128:(blk + 1) * 128], xT[:, kc, :], start=(kc == 0), stop=(kc == 7)),
                         reads=[W, xT], writes=[pb])
            chk("h%d_fm" % hd)
            pvg = pAB[:, :].rearrange("p (s c) -> p s c", c=256)
            for s in range(4):
                for kc in range(8):
                    f.op(pe, lambda: Tn.matmul(pvg[:, s, :], xT[:, kc, s * 128:(s + 1) * 128], W[:, kc, 256:512], start=(kc == 0), stop=False),
                         reads=[W, xT], writes=[pAB])
                chk("h%d_tm%d" % (hd, s))
                f.op(pe, lambda: Tn.matmul(pvg[:, s, :], onesb[0:1, 0:128], bhl_all[0:1, hd, 0, :], start=False, stop=False),
                     reads=[onesb, bhl_all], writes=[pAB])
                f.op(pe, lambda: Tn.matmul(pvg[:, s, :], onesb[0:1, 0:128], bhl_all[0:1, hd, 1, :], start=False, stop=True),
                     reads=[onesb, bhl_all], writes=[pAB])


        def partB(hd):
            is_hg = hd < 4
            h = hd % 4
            W = Whd[hd % 2]
            brow = browb[hd % 2]
            pvg = pAB[:, :].rearrange("p (s c) -> p s c", c=256)
            qa, ka, qb, kb = ba[0], ba[1], ba[2], ba[3]
            kT_for_tok = kb if is_hg else ka
            q_inter = qb if is_hg else qa
            NV = 128 if is_hg else 129
            vtok = vtokb[hd % 2]
            gw = gwb[hd % 2]
            chk("h%d_proj" % hd)
            qa, ka, qb, kb = ba[0], ba[1], ba[2], ba[3]
            if is_hg:
                q32, sig, lf, kin, Bc, Dd = fa
                c = h * 4
                f.op(act, lambda: Sx.activation(q32[:, :], p2[:, :], AF.Silu, bias=pp[:, c:c + 1]), reads=[p2, pp], writes=[q32])
                f.op(dve, lambda: V.tensor_copy(vtok[:, :, 0:128], pvg[:, :, 0:128]), reads=[pAB], writes=[vtok])
                f.op(act, lambda: Sx.activation(gw[:, :, :], pvg[:, :, 128:256], AF.Silu), reads=[pAB], writes=[gw])
                f.op(pool, lambda: G.tensor_tensor(gw[:, :, :], gw[:, :, :], hgw[:, h * 128:(h + 1) * 128].unsqueeze(1).to_broadcast([128, 4, 128]), ALU.mult),
                     reads=[gw, hgw], writes=[gw])
                f.op(act, lambda: Sx.activation(sig[:, :], p3[:, :], AF.Sigmoid, bias=pp[:, c + 1:c + 2]), reads=[p3, pp], writes=[sig])
                f.op(act, lambda: Sx.activation(lf[:, :], sig[:, :], AF.Ln, bias=pd[:, h * 3:h * 3 + 1], scale=pd[:, h * 3 + 1:h * 3 + 2]), reads=[sig, pd], writes=[lf])
                f.op(dve, lambda: V.tensor_scalar(kin[:, :], sig[:, :], pd[:, h * 3 + 2:h * 3 + 3], pd[:, h * 3 + 1:h * 3 + 2], ALU.mult, ALU.add), reads=[sig, pd], writes=[kin])
                f.op(dve, lambda: V.tensor_tensor_scan(Bc[:, :], ones[:, :], lf[:, :], 0.0, ALU.mult, ALU.add), reads=[ones, lf], writes=[Bc])
                Bv = Bc[:, :].rearrange("p (c t) -> p c t", t=64)
                f.op(dve, lambda: V.memset(sm2[:, 0:8], 0.0), writes=[sm2])
                f.op(dve, lambda: V.tensor_copy(sm2[:, 1:8], Bv[:, 0:7, 63]), reads=[Bc], writes=[sm2])
                f.op(dve, lambda: V.tensor_tensor(sm2[:, 8:16], Bv[:, :, 31], sm2[:, 0:8], ALU.subtract), reads=[Bc, sm2], writes=[sm2])
                f.op(dve, lambda: V.tensor_tensor(sm2[:, 16:24], Bv[:, :, 63], Bv[:, :, 31], ALU.subtract), reads=[Bc], writes=[sm2])
                f.op(dve, lambda: V.tensor_tensor(sm2[:, 24:32], Bv[:, :, 63], sm2[:, 0:8], ALU.subtract), reads=[Bc, sm2], writes=[sm2])
                f.op(act, lambda: Sx.activation(sm2[:, 8:32], sm2[:, 8:32], AF.Exp), reads=[sm2], writes=[sm2])
                Dv = Dd[:, :].rearrange("p (c t) -> p c t", t=64)
                f.op(dve, lambda: V.tensor_tensor(Dv, Bv, Bv[:, :, 31:32].to_broadcast([128, 8, 64]), ALU.subtract), reads=[Bc], writes=[Dd])
                E1, E1i = lf, sig
                f.op(act, lambda: Sx.activation(E1[:, :], Dd[:, :], AF.Exp), reads=[Dd], writes=[E1])
                f.op(act, lambda: Sx.activation(E1i[:, :], Dd[:, :], AF.Exp, scale=-1.0), reads=[Dd], writes=[E1i])
                f.op(dve, lambda: V.tensor_tensor(E1[:, :], q32[:, :], E1[:, :], ALU.mult), reads=[q32, E1], writes=[E1])
                f.op(dve, lambda: V.tensor_tensor(E1i[:, :], kin[:, :], E1i[:, :], ALU.mult), reads=[kin, E1i], writes=[E1i])
                f.op(pool, lambda: G.tensor_copy(qa[:, :], E1[:, :]), reads=[E1], writes=[qa])
                f.op(pool, lambda: G.tensor_copy(ka[:, :], E1i[:, :]), reads=[E1i], writes=[ka])
                f.op(dve, lambda: V.tensor_tensor(qb[:, :].rearrange("p (c t) -> p c t", t=64), E1[:, :].rearrange("p (c t) -> p c t", t=64),
                                                  sm2[:, 8:16].unsqueeze(2).to_broadcast([128, 8, 64]), ALU.mult), reads=[E1, sm2], writes=[qb])
                f.op(dve, lambda: V.tensor_tensor(kb[:, :].rearrange("p (c t) -> p c t", t=64), E1i[:, :].rearrange("p (c t) -> p c t", t=64),
                                                  sm2[:, 16:24].unsqueeze(2).to_broadcast([128, 8, 64]), ALU.mult), reads=[E1i, sm2], writes=[kb])
                kT_for_tok = kb
                q_inter = qb
                NV = 128
            else:
                cb0 = 16 + h * 12
                for s in range(4):
                    f.op(dve, lambda: V.tensor_scalar(vtok[:, s, 0:128], pvg[:, s, 0:128], eftok[:, s, 0, h:h + 1], None, ALU.mult), reads=[pAB, eftok], writes=[vtok])
                f.op(dve, lambda: V.tensor_copy(vtok[:, :, 128], eftok[:, :, 0, h]), reads=[eftok], writes=[vtok])
                f.op(act, lambda: Sx.activation(gw[:, :, :], pvg[:, :, 128:256], AF.Sigmoid), reads=[pAB], writes=[gw])
                f.op(pool, lambda: G.tensor_tensor(gw[:, :, :], gw[:, :, :], mlw[:, h * 128:(h + 1) * 128].unsqueeze(1).to_broadcast([128, 4, 128]), ALU.mult),
                     reads=[gw, mlw], writes=[gw])
                for blk, pb, dst in ((0, p2, qa), (1, p3, ka)):
                    hb = h * 2 + blk
                    ubm = ubs[blk]
                    ubmh = ubs_h[blk]
                    f.op(dve, lambda: V.tensor_copy(ubmh[:, 0:3], hml[hb][:, :]), reads=[hml[hb]], writes=[ubmh])
                    f.op(act, lambda: Sx.activation(ubm[:, 3:515], pb[:, :], AF.Identity, bias=pp[:, cb0 + blk:cb0 + blk + 1]), reads=[pb, pp], writes=[ubm])
                    f.op(pool, lambda: G.tensor_copy(hml[hb][:, :], ubm[:, 512:515]), reads=[ubm], writes=[hml[hb]])
                    acc = fa[blk]
                    wc = cb0 + 2 + blk * 4
                    f.op(dve, lambda: V.tensor_scalar(acc[:, :], ubm[:, 3:515], pp[:, wc + 3:wc + 4], pp[:, cb0 + 10 + blk:cb0 + 11 + blk], ALU.mult, ALU.add),
                         reads=[ubm, pp], writes=[acc])
                    for j in range(3):
                        f.op(dve, lambda: V.scalar_tensor_tensor(acc[:, :], ubm[:, j:j + 512], pp[:, wc + j:wc + j + 1], acc[:, :], ALU.mult, ALU.add),
                             reads=[ubm, ubmh, pp, acc], writes=[acc])
                    f.op(act, lambda: Sx.activation(acc[:, :], acc[:, :], AF.Silu), reads=[acc], writes=[acc])
                    if blk == 0:
                        f.op(dve, lambda: V.tensor_scalar(dst[:, :], acc[:, :], 128.0 ** -0.5, None, ALU.mult), reads=[acc], writes=[dst])
                    else:
                        f.op(pool, lambda: G.tensor_copy(dst[:, :], acc[:, :]), reads=[acc], writes=[dst])
                kT_for_tok = ka
                q_inter = qa
                NV = 129


        def partC1(hd):
            is_hg = hd < 4
            h = hd % 4
            W = Whd[hd % 2]
            brow = browb[hd % 2]
            pvg = pAB[:, :].rearrange("p (s c) -> p s c", c=256)
            qa, ka, qb, kb = ba[0], ba[1], ba[2], ba[3]
            kT_for_tok = kb if is_hg else ka
            q_inter = qb if is_hg else qa
            NV = 128 if is_hg else 129
            vtok = vtokb[hd % 2]
            gw = gwb[hd % 2]
            chk("h%d_elem" % hd)
            p7v = p7b[:, 0:512].rearrange("p (s d) -> p s d", d=128)
            for s in range(4):
                f.op(pe, lambda: Tn.transpose(p7v[:, s, :], kT_for_tok[:, s * 128:(s + 1) * 128], identb[:, :]), reads=[kT_for_tok, identb], writes=[p7])
            f.op(act, lambda: Sx.copy(ktok[:, :, :], p7v), reads=[p7], writes=[ktok])
            chk("h%d_ktok" % hd)
            p4v = p4[:, :].rearrange("p (s t) -> p s t", t=128)
            for s in range(4):
                f.op(pe, lambda: Tn.matmul(p4v[:, s, :], ka[:, s * 128:(s + 1) * 128], qa[:, s * 128:(s + 1) * 128], start=True, stop=True),
                     reads=[ka, qa], writes=[p4])
            f.op(dve, lambda: V.tensor_tensor(attb[:, :, :], p4v, maskT.unsqueeze(1).to_broadcast([128, 4, 128]), ALU.mult), reads=[p4, cst], writes=[attb])
            chk("h%d_att" % hd)
            pbanks = [[p5, p4], [p6, p7]]
            for c8 in range(8):
                pbk = pbanks[c8 % 2][(c8 // 2) // 3]
                o = ((c8 // 2) % 3) * 132
                lo = (c8 % 2) * 64
                chk("h%d_P%d" % (hd, c8))
                f.op(pe, lambda: Tn.matmul(pbk[:, o:o + NV], ktok[lo:lo + 64, c8 // 2, :], vtok[lo:lo + 64, c8 // 2, 0:NV], start=True, stop=True),
                     reads=[ktok, vtok], writes=[pbk])
            while pending:
                pending.pop(0)()
            chk("h%d_P" % hd)
            St = Shg[h] if is_hg else Cml[h]
            for c8 in range(8):
                pbk = pbanks[c8 % 2][(c8 // 2) // 3]
                o = ((c8 // 2) % 3) * 132
                if is_hg:
                    f.op(dve, lambda: V.tensor_copy(Sbf[:, c8, 0:NV], St[:, :]), reads=[St], writes=[Sbf])
                    dec = sm2[:, 24 + c8:25 + c8]
                else:
                    dec = decbc[:, h * 8 + c8:h * 8 + c8 + 1]
                    f.op(dve, lambda: V.tensor_scalar(Sbf[:, c8, 0:NV], St[:, :], dec, None, ALU.mult), reads=[St, decbc], writes=[Sbf])
                f.op(dve, lambda: V.scalar_tensor_tensor(St[:, :], St[:, :], dec, pbk[:, o:o + NV], ALU.mult, ALU.add), reads=[St, pbk, sm2, decbc], writes=[St])
            chk("h%d_chain" % hd)
            def pvT(s_):
                return p5 if s_ < 2 else p6

            def pv(s_, a_, b_):
                return pvT(s_)[:, (s_ % 2) * 256 + a_:(s_ % 2) * 256 + b_]
            p5v = p5[:, :].rearrange("p (s c) -> p s c", c=256)
            p6v = p6[:, :].rearrange("p (s c) -> p s c", c=256)
            f.op(pool, lambda: G.tensor_tensor(qpA[:, :], q_inter[:, :], mskA[:, :], ALU.mult), reads=[q_inter, mskA], writes=[qpA])
            f.op(pool, lambda: G.tensor_tensor(qpB[:, :], q_inter[:, :], mskB[:, :], ALU.mult), reads=[q_inter, mskB], writes=[qpB])
            for s in range(4):
                f.op(pe, lambda: Tn.matmul(pv(s, 0, NV), attb[:, s, :], vtok[:, s, 0:NV], start=True, stop=False), reads=[attb, vtok], writes=[pvT(s)])
                f.op(pe, lambda: Tn.matmul(pv(s, 0, NV), qpA[:, s * 128:s * 128 + 128], Sbf[:, 2 * s, 0:NV], start=False, stop=False),
                     reads=[qpA, Sbf], writes=[pvT(s)])
                f.op(pe, lambda: Tn.matmul(pv(s, 0, NV), qpB[:, s * 128:s * 128 + 128], Sbf[:, 2 * s + 1, 0:NV], start=False, stop=True),
                     reads=[qpB, Sbf], writes=[pvT(s)])

        def partC2(hd):
            is_hg = hd < 4
            h = hd % 4
            W = Whd[hd % 2]
            brow = browb[hd % 2]
            pvg = pAB[:, :].rearrange("p (s c) -> p s c", c=256)
            qa, ka, qb, kb = ba[0], ba[1], ba[2], ba[3]
            kT_for_tok = kb if is_hg else ka
            q_inter = qb if is_hg else qa
            NV = 128 if is_hg else 129
            vtok = vtokb[hd % 2]
            gw = gwb[hd % 2]
            pv = lambda s_, a_, b_: (p5 if s_ < 2 else p6)[:, (s_ % 2) * 256 + a_:(s_ % 2) * 256 + b_]
            pvT = lambda s_: p5 if s_ < 2 else p6
            p5v = p5[:, :].rearrange("p (s c) -> p s c", c=256)
            p6v = p6[:, :].rearrange("p (s c) -> p s c", c=256)
            chk("h%d_omm" % hd)
            if is_hg:
                for s in range(4):
                    f.op(act, lambda: Sx.activation(junk[:, :], pv(s, 0, 128), AF.Square, accum_out=sm[:, 20 + s:21 + s]), reads=[pvT(s)], writes=[junk, sm])
                f.op(dve, lambda: V.tensor_tensor(gw[:, 0:2, :], p5v[:, :, 0:128], gw[:, 0:2, :], ALU.mult), reads=[p5, gw], writes=[gw])
                f.op(dve, lambda: V.tensor_tensor(gw[:, 2:4, :], p6v[:, :, 0:128], gw[:, 2:4, :], ALU.mult), reads=[p6, gw], writes=[gw])
                f.op(dve, lambda: V.tensor_scalar(sm[:, 24:28], sm[:, 20:24], 1.0 / 128.0, EPS, ALU.mult, ALU.add), reads=[sm], writes=[sm])
                f.op(act, lambda: Sx.activation(sm[:, 24:28], sm[:, 24:28], AF.Ln), reads=[sm], writes=[sm])
                f.op(act, lambda: Sx.activation(sm[:, 24:28], sm[:, 24:28], AF.Exp, scale=-0.5), reads=[sm], writes=[sm])
                for s in range(4):
                    f.op(dve, lambda: V.tensor_scalar(ybf[:, s, :], gw[:, s, :], sm[:, 24 + s:25 + s], None, ALU.mult), reads=[sm, gw], writes=[ybf])
            else:
                f.op(act, lambda: Sx.activation(sm[:, 20:22], p5v[:, :, 128], AF.Abs), reads=[p5], writes=[sm])
                f.op(act, lambda: Sx.activation(sm[:, 22:24], p6v[:, :, 128], AF.Abs), reads=[p6], writes=[sm])
                f.op(dve, lambda: V.tensor_tensor(sm[:, 20:24], sm[:, 20:24], eftok[:, :, 1, h], ALU.max), reads=[sm, eftok], writes=[sm])
                f.op(dve, lambda: V.reciprocal(sm[:, 24:28], sm[:, 20:24]), reads=[sm], writes=[sm])
                f.op(dve, lambda: V.tensor_tensor(h32[:, 0:2, :], p5v[:, :, 0:128], sm[:, 24:26].unsqueeze(2).to_broadcast([128, 2, 128]), ALU.mult),
                     reads=[p5, sm], writes=[h32])
                f.op(dve, lambda: V.tensor_tensor(h32[:, 2:4, :], p6v[:, :, 0:128], sm[:, 26:28].unsqueeze(2).to_broadcast([128, 2, 128]), ALU.mult),
                     reads=[p6, sm], writes=[h32])
                stv = sm[:, 28:52].rearrange("p (s k) -> p s k", k=6)
                for s in range(4):
                    f.op(dve, lambda: V.bn_stats(stv[:, s, :], h32[:, s, :]), reads=[h32], writes=[sm])
                    f.op(dve, lambda: V.bn_aggr(sm[:, 52 + 2 * s:54 + 2 * s], stv[:, s, :]), reads=[sm], writes=[sm])
                mvv = sm[:, 52:60].rearrange("p (s k) -> p s k", k=2)
                f.op(dve, lambda: V.tensor_scalar(sm[:, 60:64], mvv[:, :, 1], EPS, None, ALU.add), reads=[sm], writes=[sm])
                f.op(act, lambda: Sx.activation(sm[:, 60:64], sm[:, 60:64], AF.Ln), reads=[sm], writes=[sm])
                f.op(act, lambda: Sx.activation(sm[:, 60:64], sm[:, 60:64], AF.Exp, scale=-0.5), reads=[sm], writes=[sm])
                f.op(dve, lambda: V.tensor_tensor(gw[:, :, :], gw[:, :, :], sm[:, 60:64].unsqueeze(2).to_broadcast([128, 4, 128]), ALU.mult), reads=[gw, sm], writes=[gw])
                for s in range(4):
                    f.op(dve, lambda: V.scalar_tensor_tensor(ybf[:, s, :], h32[:, s, :], mvv[:, s, 0:1], gw[:, s, :], ALU.subtract, ALU.mult),
                         reads=[h32, sm, gw], writes=[ybf])

            def ytr(hd_=hd):
                p7w = p7b[:, 512:1024].rearrange("p (s d) -> p s d", d=128)
                for s in range(4):
                    f.op(pe, lambda: Tn.transpose(p7w[:, s, :], ybf[:, s, :], identb[:, :]), reads=[ybf, identb], writes=[p7])
                f.op(act, lambda: Sx.copy(yT[:, hd_, :], p7b[:, 512:1024]), reads=[p7], writes=[yT])
            pending.append(ytr)


        pending = []
        partA(0)
        partB(0)
        for hd in range(8):
            if hd + 1 < 8:
                partA(hd + 1)
            partC1(hd)
            if hd + 1 < 8:
                partB(hd + 1)
            partC2(hd)
        while pending:
            pending.pop(0)()

        chk("mixer")
        if "yT" in dumps and it == ntiles - 1:
            dump("yT", yT, yT[:, :, :], [128, 8, 512])
        f.dma(sp, wsqb[1], wsqb[1][:, :, :], wsq_st, wsq_s[1].rearrange("p (k c) -> p k c", c=1024))
        proj_res_ln(wsqb[0], yT, 0)
        if "x1" in dumps and it == ntiles - 1:
            pass
        transposes_to_xT(4, None)
        chk("ln1")
        for cbk in range(8):
            pb = p2 if cbk % 2 == 0 else p3
            for kc in range(8):
                f.op(pe, lambda: Tn.matmul(pb[:, :], wsqb[1][:, kc, cbk * 128:(cbk + 1) * 128], xT[:, kc, :], start=(kc == 0), stop=(kc == 7)),
                     reads=[wsqb[1], xT], writes=[pb])
            if cbk % 2 == 0:
                f.op(dve, lambda: V.tensor_copy(qTc[:, cbk, :], pb[:, :]), reads=[pb], writes=[qTc])
            else:
                f.op(act, lambda: Sx.copy(qTc[:, cbk, :], pb[:, :]), reads=[pb], writes=[qTc])
        f.dma(sp, wsqb[0], wsqb[0][:, :, :], wsq_st, wsq_s[2].rearrange("p (k c) -> p k c", c=1024))
        for h in range(4):
            for mc in range(2):
                pb = p2 if mc == 0 else p3
                for j in range(2):
                    f.op(pe, lambda: Tn.matmul(pb[:, :], KT[:, 2 * h + j, mc * 128:(mc + 1) * 128], qTc[:, 2 * h + j, :], start=(j == 0), stop=(j == 1)),
                         reads=[KT, qTc], writes=[pb])
                f.op(act, lambda: Sx.activation(ETb[mc][:, :], pb[:, :], AF.Exp, scale=1.0 / 16.0), reads=[pb], writes=[ETb[mc]])
            for mc in range(2):
                f.op(pe, lambda: Tn.matmul(p4[:, :], onesb[:, :], ETb[mc][:, :], start=(mc == 0), stop=(mc == 1)), reads=[onesb, ETb[mc]], writes=[p4])
            rden = fa[0]
            f.op(dve, lambda: V.reciprocal(rden[:, :], p4[:, :]), reads=[p4], writes=[rden])
            for j in range(2):
                pb = p5 if j == 0 else p6
                for mc in range(2):
                    f.op(pe, lambda: Tn.matmul(pb[:, :], Vm[:, mc, (2 * h + j) * 128:(2 * h + j + 1) * 128], ETb[mc][:, :], start=(mc == 0), stop=(mc == 1)),
                         reads=[Vm, ETb[mc]], writes=[pb])
                f.op(dve, lambda: V.tensor_tensor(yT[:, 2 * h + j, :], pb[:, :], rden[:, :], ALU.mult), reads=[pb, rden], writes=[yT])
        proj_res_ln(wsqb[0], yT, 1)
        if "x2" in dumps and it == ntiles - 1:
            pass
        transposes_to_xT(4, None)
        chk("ca")
        pdn = [pAB, pAB, p5, p6]
        def pdn_ap(s):
            return pAB[:, s * 512:(s + 1) * 512] if s < 2 else pdn[s][:, :]
        def ffn_load(g):
            f.dma(sp, wupb[g % 2], wupb[g % 2][:, :, :, :], wup_st, wup_s[g].rearrange("p (j k c) -> p j k c", j=2, k=8))
            f.dma(sp, wdnb[g % 3], wdnb[g % 3][:, :, :], wdn_st, wdn_s[0, g].rearrange("p (j c) -> p j c", j=2))

        def ffn_up(j):
            g, jj = j // 2, j % 2
            if jj == 0 and g + 1 < 11:
                ffn_load(g + 1)
            wu = wupb[g % 2]
            for gv in range(2):
                pb = pup[(2 * j + gv) % 4]
                for kc in range(8):
                    f.op(pe, lambda: Tn.matmul(pb[:, :], wu[:, jj, kc, gv * 128:(gv + 1) * 128], xT[:, kc, :], start=(kc == 0), stop=(kc == 7)),
                         reads=[wu, xT], writes=[pb])

        def ffn_ew(j):
            accs = []
            for gv in range(2):
                pb = pup[(2 * j + gv) % 4]
                ub = ubs[(j % 2) * 2 + gv]
                bidx = j + 22 * gv
                ubh = ubs_h[(j % 2) * 2 + gv]
                f.op(dve, lambda: V.tensor_copy(ubh[:, 0:2], hff[bidx][:, :]), reads=[hff[bidx]], writes=[ubh])
                f.op(act, lambda: Sx.copy(ub[:, 2:514], pb[:, :]), reads=[pb], writes=[ub])
                f.op(pool, lambda: G.tensor_copy(hff[bidx][:, :], ub[:, 512:514]), reads=[ub], writes=[hff[bidx]])
                acc = fa[(j % 2) * 2 + gv]
                pc = 64 + bidx * 4
                f.op(act, lambda: Sx.activation(acc[:, :], pb[:, :], AF.Identity, bias=pp[:, pc + 3:pc + 4], scale=pp[:, pc + 2:pc + 3]), reads=[pb, pp], writes=[acc])
                for t in range(2):
                    f.op(dve, lambda: V.scalar_tensor_tensor(acc[:, :], ub[:, t:t + 512], pp[:, pc + t:pc + t + 1], acc[:, :], ALU.mult, ALU.add),
                         reads=[ub, ubh, pp, acc], writes=[acc])
                accs.append(acc)
            f.op(act, lambda: Sx.activation(accs[0][:, :], accs[0][:, :], AF.Gelu_apprx_tanh), reads=[accs[0]], writes=[accs[0]])
            f.op(dve, lambda: V.tensor_tensor(hTj[j][:, :], accs[0][:, :], accs[1][:, :], ALU.mult), reads=accs, writes=[hTj[j]])

        def ffn_down(j):
            g, jj = j // 2, j % 2
            wd = wdnb[g % 3]
            for s in range(4):
                f.op(pe, lambda: Tn.matmul(pdn_ap(s), hTj[j][:, s * 128:(s + 1) * 128], wd[:, jj, :], start=(j == 0), stop=(j == 21)),
                     reads=[hTj[j], wd], writes=[pdn[s]])

        ffn_load(0)
        ffn_up(0)
        ffn_up(1)
        for j in range(22):
            ffn_ew(j)
            if j + 2 < 22:
                ffn_up(j + 2)
            ffn_down(j)
        f.dma(sp, lnp, lnp[:, :, :], None,
              rows_d[0:1, R_LN + 2 * 2048:R_LN + 3 * 2048].rearrange("o (a d) -> o a d", a=2).partition_broadcast(128))
        for s in range(4):
            f.op(dve, lambda: V.scalar_tensor_tensor(xr[s][:, 0:512], xr[s][:, 0:512], ALPHA, pdn_ap(s), ALU.mult, ALU.add), reads=[xr[s], pdn[s]], writes=[xr[s]])
        for g0 in range(2):
            f.dma(sp, wdnb[g0 % 3], wdnb[g0 % 3][:, :, :], wdn_st, wdn_s[1, g0].rearrange("p (j c) -> p j c", j=2))
        for g in range(11):
            if g + 2 < 11:
                f.dma(sp, wdnb[(g + 2) % 3], wdnb[(g + 2) % 3][:, :, :], wdn_st, wdn_s[1, g + 2].rearrange("p (j c) -> p j c", j=2))
            wd = wdnb[g % 3]
            for jj in range(2):
                j = 2 * g + jj
                for s in range(4):
                    f.op(pe, lambda: Tn.matmul(pdn_ap(s), hTj[j][:, s * 128:(s + 1) * 128], wd[:, jj, :], start=(j == 0), stop=(j == 21)),
                         reads=[hTj[j], wd], writes=[pdn[s]])
        for s in range(4):
            f.op(dve, lambda: V.scalar_tensor_tensor(xr[s][:, 512:1024], xr[s][:, 512:1024], ALPHA, pdn_ap(s), ALU.mult, ALU.add), reads=[xr[s], pdn[s]], writes=[xr[s]])
            layernorm_rows(s, 2)
            if s > 0:
                layernorm_apply(s - 1)
                f.dma(sp, out_t, out_d[t0 + (s - 1) * 128:t0 + s * 128, :], xr[s - 1], xr[s - 1][:, :])
        layernorm_apply(3)
        f.dma(sp, out_t, out_d[t0 + 3 * 128:t0 + 4 * 128, :], xr[3], xr[3][:, :])


def host_prep(inp):
    w_in = inp["w_in"][0]
    b_in = inp["b_in"][0]

    def tile_k(w):
        return w.reshape(8, 128, -1).transpose(1, 0, 2)
    whd = np.empty((8, 128, 8, 512), np.float32)
    brow = np.empty((8, 256), np.float32)
    for h in range(4):
        cols = [slice(0 + h * 128, 128 + h * 128), slice(512 + h * 128, 640 + h * 128), slice(1024 + h * 128, 1152 + h * 128), slice(1536 + h * 128, 1664 + h * 128)]
        whd[h] = np.concatenate([tile_k(w_in[:, c]) for c in cols], axis=2)
        brow[h] = np.concatenate([b_in[cols[2]], b_in[cols[3]]])
        cols = [slice(2048 + h * 128, 2176 + h * 128), slice(2560 + h * 128, 2688 + h * 128), slice(3072 + h * 128, 3200 + h * 128), slice(3584 + h * 128, 3712 + h * 128)]
        whd[4 + h] = np.concatenate([tile_k(w_in[:, c]) for c in cols], axis=2)
        brow[4 + h] = np.concatenate([b_in[cols[2]], b_in[cols[3]]])
    wg = tile_k(w_in[:, 4096:4104]).reshape(128, 64)
    wsq = np.stack([tile_k(inp["w_out"][0]), tile_k(inp["ca_wq"][0]), tile_k(inp["ca_wo"][0])]).reshape(3, 128, 8192)
    wkv = inp["ca_wkv"][0]
    wkv_t = np.stack([tile_k(wkv[:, :1024]), tile_k(wkv[:, 1024:])]).reshape(2, 128, 8192)
    wu = inp["ffn_w_up"][0]
    wup = np.empty((11, 128, 2, 8, 256), np.float32)
    for j in range(22):
        blk = np.concatenate([tile_k(wu[:, j * 128:(j + 1) * 128]), tile_k(wu[:, 2816 + j * 128:2816 + (j + 1) * 128])], axis=2)
        wup[j // 2, :, j % 2] = blk
    wd = inp["ffn_w_down"][0]
    wdn = np.empty((2, 11, 128, 2, 512), np.float32)
    for j in range(22):
        for hf in range(2):
            wdn[hf, j // 2, :, j % 2] = wd[j * 128:(j + 1) * 128, hf * 512:(hf + 1) * 512]
    pp = np.zeros((128, NPP), np.float32)
    lbl = inp["hg_lb_logits"]
    cw = inp["ml_conv_w"][0]; cbias = inp["ml_conv_b"][0]
    for h in range(4):
        sl = slice(h * 128, (h + 1) * 128)
        pp[:, h * 4 + 0] = b_in[0 + h * 128:128 + h * 128]
        pp[:, h * 4 + 1] = b_in[512 + h * 128:640 + h * 128]
        pp[:, h * 4 + 2] = lbl[0, sl]
        pp[:, h * 4 + 3] = lbl[1, sl]
        c0 = 16 + h * 12
        pp[:, c0 + 0] = b_in[2048 + h * 128:2176 + h * 128]
        pp[:, c0 + 1] = b_in[2560 + h * 128:2688 + h * 128]
        for blk in range(2):
            csl = slice(blk * 512 + h * 128, blk * 512 + (h + 1) * 128)
            for j in range(4):
                pp[:, c0 + 2 + blk * 4 + j] = cw[j, csl]
            pp[:, c0 + 10 + blk] = cbias[csl]
    fw = inp["ffn_conv_w"][0]; fb = inp["ffn_conv_b"][0]
    for b in range(44):
        sl = slice(b * 128, (b + 1) * 128)
        for j in range(3):
            pp[:, 64 + b * 4 + j] = fw[j, sl]
        pp[:, 64 + b * 4 + 3] = fb[sl]
    pp[0:4, 240] = b_in[4096:4100]
    pp[0:4, 241] = b_in[4100:4104]
    rows = np.concatenate([inp["hg_norm_w"][0], inp["ml_norm_w"][0], inp["ln1_g"][0], inp["ln1_b"][0], inp["ln2_g"][0], inp["ln2_b"][0],
                           inp["ln3_g"][0], inp["ln3_b"][0], brow.reshape(-1)]).astype(np.float32)[None, :]
    cst = np.zeros((128, 512), np.float32)
    cst[:, 0:128] = np.eye(128, dtype=np.float32)
    idx = np.arange(128)
    cst[:, 128:256] = ((idx[:, None] // 64 == idx[None, :] // 64) & (idx[:, None] <= idx[None, :])).astype(np.float32)
    for k in range(4):
        cst[k, 256 + k * 8:256 + (k + 1) * 8] = 1.0
    shared = dict(whd=np.ascontiguousarray(whd.reshape(8, 128, 4096)), wg=np.ascontiguousarray(wg), wsq=np.ascontiguousarray(wsq),
                  wkv=np.ascontiguousarray(wkv_t), wup=np.ascontiguousarray(wup.reshape(11, 128, 4096)),
                  wdn=np.ascontiguousarray(wdn.reshape(2, 11, 128, 1024)), pp=pp, rows=np.ascontiguousarray(rows), cst=cst)
    return shared


_NC_CACHE = {}


def kernel(**inputs):
    inp = {k: np.asarray(v) for k, v in inputs.items()}
    shared = host_prep(inp)
    if "nc" not in _NC_CACHE:
        _NC_CACHE["nc"] = build()
    nc = _NC_CACHE["nc"]
    in_maps = []
    for b in range(8):
        m = dict(shared)
        m["x"] = np.ascontiguousarray(inp["x"][b])
        m["mem"] = np.ascontiguousarray(inp["mem"][b])
        in_maps.append(m)
    res = run_bass_kernel_spmd(nc, in_maps, core_ids=list(range(8)))
    return np.stack([np.asarray(r["out"]) for r in res.results]).astype(np.float32)
```
